# Optimizing a Trainium2 kernel written in Bass

```python
import jax, jax.numpy as jnp
from jax import lax
import numpy as np

D_MODEL = 1024
BATCH = 4
SEQ = 4096
DEPTH = 4

RET_HEADS = 4
RET_DK = 64
RET_DV = 128
RET_CHUNK = 128
ROPE_BASE = 10000.0
POOL_GROUPS = 4
POOL_DIM = 128
POOL_WINDOWS = (2, 4, 8, 16)
GDN_HEADS = 4
GDN_DK = 128
GDN_DV = 128
GDN_CONV = 4
GDN_CHUNK = 64
BRANCH_W = 512
N_BRANCH = 3
MOE_GROUPS = 4
MOE_EXPERTS_PER_GROUP = 4
MOE_TOP_K = 2
MOE_HIDDEN = 512
N_EXPERTS = MOE_GROUPS * MOE_EXPERTS_PER_GROUP
DEEPNORM_ALPHA = (2 * DEPTH) ** 0.25
DEEPNORM_BETA = (8 * DEPTH) ** -0.25
LN_EPS = 1e-5
RMS_EPS = 1e-6
IN_SIZES = (RET_HEADS * RET_DK, RET_HEADS * RET_DK, RET_HEADS * RET_DV, RET_HEADS * RET_DV,
            POOL_GROUPS * POOL_DIM,
            GDN_HEADS * GDN_DK, GDN_HEADS * GDN_DK, GDN_HEADS * GDN_DV, GDN_HEADS * GDN_DV,
            GDN_HEADS, GDN_HEADS,
            N_BRANCH * D_MODEL)
IN_COLS = sum(IN_SIZES)

kernel_name = "hybrid_retention_pool_gdn_hiermoe_deepnorm"


def layer_norm(x, g, b, dtype):
    xf = x.astype(jnp.float32)
    mu = jnp.mean(xf, axis=-1, keepdims=True)
    xc = xf - mu
    var = jnp.mean(xc * xc, axis=-1, keepdims=True)
    return (xc * lax.rsqrt(var + LN_EPS) * g.astype(jnp.float32) + b.astype(jnp.float32)).astype(dtype)


def rotary(x, pos):
    half = x.shape[-1] // 2
    inv_freq = ROPE_BASE ** (-jnp.arange(half, dtype=jnp.float32) / half)
    ang = pos[:, None] * inv_freq[None, :]
    cos = jnp.cos(ang)[None, :, None, :]
    sin = jnp.sin(ang)[None, :, None, :]
    x1, x2 = x[..., :half], x[..., half:]
    return jnp.concatenate([x1 * cos - x2 * sin, x1 * sin + x2 * cos], axis=-1)


def chunk_retention(q, k, v, log_gamma):
    B, T, H, dk = q.shape
    dv = v.shape[-1]
    C = RET_CHUNK
    N = T // C
    q = q.reshape(B, N, C, H, dk)
    k = k.reshape(B, N, C, H, dk)
    v = v.reshape(B, N, C, H, dv)
    idx = jnp.arange(C, dtype=jnp.float32)
    rel = idx[:, None] - idx[None, :]
    decay = jnp.where(rel[None] >= 0, jnp.exp(jnp.maximum(rel, 0.0)[None] * log_gamma[:, None, None]), 0.0)
    scores = jnp.einsum('bnihd,bnjhd->bnhij', q, k) * decay[None, None]
    inner = jnp.einsum('bnhij,bnjhe->bnihe', scores, v)
    zeta = jnp.exp((C - 1 - idx)[:, None] * log_gamma[None, :])
    kv = jnp.einsum('bnjhd,bnjhe->nbhde', k * zeta[:, :, None], v)
    chunk_decay = jnp.exp(C * log_gamma)[:, None, None]

    def step(S, kv_n):
        return S * chunk_decay + kv_n, S

    _, S_prev = lax.scan(step, jnp.zeros((B, H, dk, dv), jnp.float32), kv)
    xi = jnp.exp((idx + 1.0)[:, None] * log_gamma[None, :])
    cross = jnp.einsum('bnihd,nbhde->bnihe', q * xi[:, :, None], S_prev)
    return (inner + cross).reshape(B, T, H, dv)


def retention_branch(q, k, v, gate, pos, log_gamma):
    B, T, _ = q.shape
    f32 = jnp.float32
    q = rotary(q.astype(f32).reshape(B, T, RET_HEADS, RET_DK), pos)
    k = rotary(k.astype(f32).reshape(B, T, RET_HEADS, RET_DK), pos) * (RET_DK ** -0.5)
    v = v.astype(f32).reshape(B, T, RET_HEADS, RET_DV)
    o = chunk_retention(q, k, v, log_gamma)
    mu = jnp.mean(o, axis=-1, keepdims=True)
    oc = o - mu
    o = oc * lax.rsqrt(jnp.mean(oc * oc, axis=-1, keepdims=True) + LN_EPS)
    return jax.nn.silu(gate.astype(f32)) * o.reshape(B, T, RET_HEADS * RET_DV)


def pool_branch(u, pool_w, pool_scale):
    B, T, _ = u.shape
    u = u.astype(jnp.float32).reshape(B, T, POOL_GROUPS, POOL_DIM)
    cs = jnp.concatenate([jnp.zeros((B, 1, POOL_GROUPS, POOL_DIM), jnp.float32), jnp.cumsum(u, axis=1)], axis=1)
    t = jnp.arange(T)
    outs = []
    for gi, w in enumerate(POOL_WINDOWS):
        lo = jnp.maximum(t + 1 - w, 0)
        s = cs[:, 1:, gi] - cs[:, lo, gi]
        cnt = (t + 1 - lo).astype(jnp.float32)
        outs.append(s / cnt[None, :, None] - u[:, :, gi])
    p = jnp.stack(outs, axis=2)
    y = jnp.einsum('btgc,gcd->btgd', p, pool_w.astype(jnp.float32))
    return y.reshape(B, T, POOL_GROUPS * POOL_DIM) * pool_scale.astype(jnp.float32)


def causal_dwconv(x, w):
    K = w.shape[0]
    return lax.conv_general_dilated(x, w[:, None, :], window_strides=(1,), padding=[(K - 1, 0)],
                                    dimension_numbers=('NWC', 'WIO', 'NWC'), feature_group_count=x.shape[-1])


def l2norm(x):
    return x * lax.rsqrt(jnp.sum(x * x, axis=-1, keepdims=True) + RMS_EPS)


def gated_delta_rule(q, k, v, g, beta):
    B, T, H, dk = q.shape
    dv = v.shape[-1]
    C = GDN_CHUNK
    N = T // C
    q = l2norm(q) * (dk ** -0.5)
    k = l2norm(k)

    def chunks(a):
        return a.reshape(B, N, C, H, -1).transpose(0, 3, 1, 2, 4)

    q, k, v = chunks(q), chunks(k), chunks(v)
    beta = beta.reshape(B, N, C, H).transpose(0, 3, 1, 2)
    g = jnp.cumsum(g.reshape(B, N, C, H).transpose(0, 3, 1, 2), axis=-1)
    idx = jnp.arange(C)
    incl = idx[:, None] >= idx[None, :]
    strict = idx[:, None] > idx[None, :]
    diff = g[..., :, None] - g[..., None, :]
    decay = jnp.where(incl, jnp.exp(jnp.where(incl, diff, 0.0)), 0.0)
    k_beta = k * beta[..., None]
    A = jnp.where(strict, jnp.einsum('bhnid,bhnjd->bhnij', k_beta, k) * decay, 0.0)
    L = A + jnp.eye(C, dtype=jnp.float32)
    rhs = jnp.concatenate([v * beta[..., None], k_beta * jnp.exp(g)[..., None]], axis=-1)
    sol = lax.linalg.triangular_solve(L, rhs, left_side=True, lower=True, unit_diagonal=True)
    u, w = sol[..., :dv], sol[..., dv:]
    attn = jnp.einsum('bhnid,bhnjd->bhnij', q, k) * decay
    g_last = g[..., -1]
    k_dec = k * jnp.exp(g_last[..., None] - g)[..., None]
    q_dec = q * jnp.exp(g)[..., None]
    xs = (jnp.moveaxis(q_dec, 2, 0), jnp.moveaxis(k_dec, 2, 0), jnp.moveaxis(u, 2, 0),
          jnp.moveaxis(w, 2, 0), jnp.moveaxis(attn, 2, 0), jnp.moveaxis(g_last, 2, 0))

    def step(S, inp):
        q_n, k_n, u_n, w_n, attn_n, gl_n = inp
        v_new = u_n - jnp.einsum('bhcd,bhde->bhce', w_n, S)
        o = jnp.einsum('bhcd,bhde->bhce', q_n, S) + jnp.einsum('bhij,bhje->bhie', attn_n, v_new)
        S = S * jnp.exp(gl_n)[..., None, None] + jnp.einsum('bhcd,bhce->bhde', k_n, v_new)
        return S, o

    _, o = lax.scan(step, jnp.zeros((B, H, dk, dv), jnp.float32), xs)
    return o.transpose(1, 0, 3, 2, 4).reshape(B, T, H, dv)


def gdn_branch(q, k, v, z, b_logit, a_logit, conv_w, A_log, dt_bias, norm_w):
    B, T, _ = q.shape
    f32 = jnp.float32
    qkv = jnp.concatenate([q, k, v], axis=-1).astype(f32)
    qkv = jax.nn.silu(causal_dwconv(qkv, conv_w.astype(f32)))
    q, k, v = jnp.split(qkv, [GDN_HEADS * GDN_DK, 2 * GDN_HEADS * GDN_DK], axis=-1)
    q = q.reshape(B, T, GDN_HEADS, GDN_DK)
    k = k.reshape(B, T, GDN_HEADS, GDN_DK)
    v = v.reshape(B, T, GDN_HEADS, GDN_DV)
    beta = jax.nn.sigmoid(b_logit.astype(f32))
    g = -jnp.exp(A_log.astype(f32)) * jax.nn.softplus(a_logit.astype(f32) + dt_bias.astype(f32))
    o = gated_delta_rule(q, k, v, g, beta)
    o = o * lax.rsqrt(jnp.mean(o * o, axis=-1, keepdims=True) + RMS_EPS) * norm_w.astype(f32)
    o = o * jax.nn.silu(z.astype(f32).reshape(B, T, GDN_HEADS, GDN_DV))
    return o.reshape(B, T, GDN_HEADS * GDN_DV)


def hier_moe(h, wc, bc, wf, bf, w_gate, w_up, w_down):
    B, T, D = h.shape
    f32 = jnp.float32
    xt = h.reshape(B * T, D)
    pc = jax.nn.softmax((xt @ wc).astype(f32) + bc.astype(f32), axis=-1)
    p_g, g_sel = lax.top_k(pc, 1)
    fine = ((xt @ wf).astype(f32) + bf.astype(f32)).reshape(-1, MOE_GROUPS, MOE_EXPERTS_PER_GROUP)
    fine_sel = jnp.take_along_axis(fine, g_sel[:, :, None], axis=1)[:, 0]
    pf = jax.nn.softmax(fine_sel, axis=-1)
    top_w, top_i = lax.top_k(pf, MOE_TOP_K)
    top_w = top_w / jnp.sum(top_w, axis=-1, keepdims=True)
    expert_id = g_sel * MOE_EXPERTS_PER_GROUP + top_i
    weights = p_g * top_w
    combine = jnp.einsum('nk,nke->ne', weights, jax.nn.one_hot(expert_id, N_EXPERTS, dtype=f32))
    out = jnp.zeros((B * T, D), f32)
    for e in range(N_EXPERTS):
        hdn = jax.nn.silu(xt @ w_gate[e]) * (xt @ w_up[e])
        out = out + combine[:, e:e + 1] * (hdn @ w_down[e])
    return out.reshape(B, T, D)


def setup_inputs(seed: int = 0) -> dict:
    key = jax.random.key(seed)
    ks = jax.random.split(key, 24)
    f32 = jnp.float32

    def nrm(k, shape, scale):
        return jax.random.normal(k, shape, f32) * scale

    b = DEEPNORM_BETA
    col_scale = np.concatenate([
        np.full(IN_SIZES[0] + IN_SIZES[1], 1.0), np.full(IN_SIZES[2], b), np.full(IN_SIZES[3], 1.0),
        np.full(IN_SIZES[4], b),
        np.full(IN_SIZES[5] + IN_SIZES[6], 1.0), np.full(IN_SIZES[7], b),
        np.full(IN_SIZES[8] + IN_SIZES[9] + IN_SIZES[10] + IN_SIZES[11], 1.0)]).astype(np.float32)
    x = jax.random.normal(ks[0], (BATCH, SEQ, D_MODEL), f32)
    w_in = nrm(ks[1], (DEPTH, D_MODEL, IN_COLS), D_MODEL ** -0.5) * jnp.asarray(col_scale)
    pool_w = nrm(ks[2], (DEPTH, POOL_GROUPS, POOL_DIM, POOL_DIM), POOL_DIM ** -0.5)
    pool_scale = 1.0 + nrm(ks[3], (DEPTH, POOL_GROUPS * POOL_DIM), 0.1)
    conv_w = nrm(ks[4], (DEPTH, GDN_CONV, 2 * GDN_HEADS * GDN_DK + GDN_HEADS * GDN_DV), GDN_CONV ** -0.5)
    A_log = jnp.log(jax.random.uniform(ks[5], (DEPTH, GDN_HEADS), f32, 1.0, 16.0))
    dt = jnp.exp(jax.random.uniform(ks[6], (DEPTH, GDN_HEADS), f32, np.log(1e-3), np.log(1e-1)))
    dt_bias = dt + jnp.log(-jnp.expm1(-dt))
    gdn_norm_w = 1.0 + nrm(ks[7], (DEPTH, GDN_DV), 0.02)
    w_branch = nrm(ks[8], (DEPTH, N_BRANCH, BRANCH_W, D_MODEL), BRANCH_W ** -0.5 * b)
    w_out = nrm(ks[9], (DEPTH, D_MODEL, D_MODEL), D_MODEL ** -0.5 * b)
    ln1_g = 1.0 + nrm(ks[10], (DEPTH, D_MODEL), 0.02)
    ln1_b = nrm(ks[11], (DEPTH, D_MODEL), 0.02)
    router_coarse_w = nrm(ks[12], (DEPTH, D_MODEL, MOE_GROUPS), D_MODEL ** -0.5)
    router_coarse_b = nrm(ks[13], (DEPTH, MOE_GROUPS), 0.01)
    router_fine_w = nrm(ks[14], (DEPTH, D_MODEL, N_EXPERTS), D_MODEL ** -0.5)
    router_fine_b = nrm(ks[15], (DEPTH, N_EXPERTS), 0.01)
    w_gate = nrm(ks[16], (DEPTH, N_EXPERTS, D_MODEL, MOE_HIDDEN), D_MODEL ** -0.5 * b)
    w_up = nrm(ks[17], (DEPTH, N_EXPERTS, D_MODEL, MOE_HIDDEN), D_MODEL ** -0.5 * b)
    w_down = nrm(ks[18], (DEPTH, N_EXPERTS, MOE_HIDDEN, D_MODEL), MOE_HIDDEN ** -0.5 * b)
    ln2_g = 1.0 + nrm(ks[19], (DEPTH, D_MODEL), 0.02)
    ln2_b = nrm(ks[20], (DEPTH, D_MODEL), 0.02)
    return {"x": x, "w_in": w_in, "pool_w": pool_w, "pool_scale": pool_scale, "conv_w": conv_w,
            "A_log": A_log, "dt_bias": dt_bias, "gdn_norm_w": gdn_norm_w, "w_branch": w_branch,
            "w_out": w_out, "ln1_g": ln1_g, "ln1_b": ln1_b, "router_coarse_w": router_coarse_w,
            "router_coarse_b": router_coarse_b, "router_fine_w": router_fine_w,
            "router_fine_b": router_fine_b, "w_gate": w_gate, "w_up": w_up, "w_down": w_down,
            "ln2_g": ln2_g, "ln2_b": ln2_b}


def reference(x, w_in, pool_w, pool_scale, conv_w, A_log, dt_bias, gdn_norm_w, w_branch, w_out,
              ln1_g, ln1_b, router_coarse_w, router_coarse_b, router_fine_w, router_fine_b,
              w_gate, w_up, w_down, ln2_g, ln2_b):
    dtype = x.dtype
    B, T, D = x.shape
    offsets = np.cumsum(IN_SIZES)[:-1].tolist()
    pos = jnp.arange(T, dtype=jnp.float32)
    log_gamma = jnp.log(1.0 - jnp.exp2(-5.0 - jnp.arange(RET_HEADS, dtype=jnp.float32)))
    for l in range(DEPTH):
        proj = jnp.einsum('btd,dc->btc', x, w_in[l])
        rq, rk, rv, rg, pu, gq, gk, gv, gz, gb, ga, gate_logits = jnp.split(proj, offsets, axis=-1)
        y_ret = retention_branch(rq, rk, rv, rg, pos, log_gamma)
        y_pool = pool_branch(pu, pool_w[l], pool_scale[l])
        y_gdn = gdn_branch(gq, gk, gv, gz, gb, ga, conv_w[l], A_log[l], dt_bias[l], gdn_norm_w[l])
        ys = jnp.stack([y_ret, y_pool, y_gdn], axis=2)
        up = jnp.einsum('btrc,rcd->btrd', ys, w_branch[l])
        gates = jax.nn.sigmoid(gate_logits.astype(jnp.float32).reshape(B, T, N_BRANCH, D))
        mixed = jnp.sum(gates * up, axis=2)
        mix_out = mixed @ w_out[l]
        x = layer_norm(DEEPNORM_ALPHA * x + mix_out, ln1_g[l], ln1_b[l], dtype)
        ffn_out = hier_moe(x, router_coarse_w[l], router_coarse_b[l], router_fine_w[l], router_fine_b[l],
                           w_gate[l], w_up[l], w_down[l])
        x = layer_norm(DEEPNORM_ALPHA * x + ffn_out, ln2_g[l], ln2_b[l], dtype)
    return x
```

```python
import os, math
import numpy as np
import ml_dtypes
import concourse.bass as bass
import concourse.mybir as mybir
from concourse.bass_utils import run_bass_kernel_spmd
from contextlib import ExitStack

F32 = mybir.dt.float32
BF16 = mybir.dt.bfloat16
I32 = mybir.dt.int32
AF = mybir.ActivationFunctionType
ALU = mybir.AluOpType
AX = mybir.AxisListType


class T_:
    __slots__ = ("name", "lw", "rd")

    def __init__(self, name=""):
        self.name = name
        self.lw = None
        self.rd = []


class Op_:
    __slots__ = ("eng", "fn", "deps", "dma", "sig", "sigidx", "dsem", "dval", "idx", "guard")

    def __init__(self, eng, fn, dma):
        self.eng = eng
        self.fn = fn
        self.dma = dma
        self.deps = set()
        self.sig = False
        self.sigidx = 0
        self.dsem = None
        self.dval = 0
        self.guard = None


class Sched:
    ENGS = ("pe", "act", "dve", "pool", "sp")

    def __init__(self, nc, n_dma_sems=12, same_engine_sync=("act", "dve", "pool")):
        self.nc = nc
        self.ops = []
        self.n_dma_sems = n_dma_sems
        self.same = set(same_engine_sync)

    def T(self, name=""):
        return T_(name)

    def add(self, eng, fn, reads=(), writes=(), dma=False):
        op = Op_(eng, fn, dma)
        op.idx = len(self.ops)
        for t in reads:
            if t.lw is not None:
                op.deps.add(t.lw)
        for t in writes:
            if t.lw is not None:
                op.deps.add(t.lw)
            for r in t.rd:
                op.deps.add(r)
        for t in reads:
            t.rd.append(op.idx)
        for t in writes:
            t.lw = op.idx
            t.rd = []
        op.deps.discard(op.idx)
        self.ops.append(op)
        return op

    def dma(self, eng, out, in_, reads=(), writes=(), **kw):
        return self.add(eng, lambda e: e.dma_start(out=out, in_=in_, **kw), reads, writes, dma=True)

    def emit(self):
        nc = self.nc
        ops = self.ops
        for op in ops:
            for d in op.deps:
                dop = ops[d]
                if dop.dma:
                    dop.sig = True
                elif dop.eng != op.eng or op.eng in self.same or op.dma:
                    dop.sig = True
        cnt = {e: 0 for e in self.ENGS}
        ndq = {e: 0 for e in self.ENGS}
        ndma = 0
        NS = self.n_dma_sems
        for op in ops:
            if op.dma:
                op.sig = True
                q = "cc" if op.dma == "cc" else op.eng
                inc = 1 if op.dma == "cc" else 16
                m = ndq.setdefault(q, 0)
                op.dsem = (q, m % NS)
                op.dval = inc * (m // NS + 1)
                if m >= NS:
                    op.guard = (op.dsem, op.dval - inc)
                ndq[q] += 1
                ndma += 1
            elif op.sig:
                cnt[op.eng] += 1
                op.sigidx = cnt[op.eng]
        self.stats = dict(n_ops=len(ops), n_dma=ndma, sig=dict(cnt))
        with ExitStack() as es:
            esem = {e: es.enter_context(nc.semaphore("s_" + e)) for e in self.ENGS}
            dsems = {}
            for e in list(ndq.keys()):
                for i in range(min(NS, ndq[e])):
                    dsems[(e, i)] = es.enter_context(nc.semaphore("d_%s_%d" % (e, i)))
            block = es.enter_context(nc.Block())

            def run(engname, e):
                waited = {}

                def wait(key, sem, val):
                    if waited.get(key, 0) >= val:
                        return
                    waited[key] = val
                    e.wait_ge(sem, val)

                for op in ops:
                    if op.eng != engname:
                        continue
                    for d in sorted(op.deps):
                        dop = ops[d]
                        if dop.dma:
                            wait(("d", dop.dsem), dsems[dop.dsem], dop.dval)
                        elif dop.eng != engname or engname in self.same or op.dma:
                            wait(("e", dop.eng), esem[dop.eng], dop.sigidx)
                    if op.guard is not None:
                        wait(("d", op.guard[0]), dsems[op.guard[0]], op.guard[1])
                    if op.fn is None:
                        continue
                    ins = op.fn(e)
                    if op.dma:
                        ins.then_inc(dsems[op.dsem], 1 if op.dma == "cc" else 16)
                    elif op.sig:
                        ins.then_inc(esem[engname], 1)

            @block.tensor
            def _(e):
                run("pe", e)

            @block.scalar
            def _(e):
                run("act", e)

            @block.vector
            def _(e):
                run("dve", e)

            @block.gpsimd
            def _(e):
                run("pool", e)

            @block.sync
            def _(e):
                run("sp", e)


ALPHA = (2 * 4) ** 0.25
LN_EPS = 1e-5
NTOK = 2048
TT = 512
NT = NTOK // TT


class Arena:
    def __init__(self, nc, es, nbytes):
        self.t = es.enter_context(nc.sbuf_tensor("arena", [128, nbytes // 4], F32))
        self.off = 0
        self.nbytes = nbytes

    def seek(self, off):
        self.off = off

    def carve(self, shape, dt):
        esz = 4 if dt == F32 else 2
        n = 1
        for s in shape:
            n *= s
        nb = (n * esz + 31) // 32 * 32
        assert self.off % 4 == 0 and self.off + nb <= self.nbytes, (self.off, nb, self.nbytes)
        ap = self.t[:, self.off // 4:(self.off + nb) // 4]
        if dt != F32:
            ap = ap.bitcast(dt)
        ap = ap[:, 0:n]
        if len(shape) == 2:
            ap = ap.rearrange("p (a b) -> p a b", a=shape[0])
        elif len(shape) == 3:
            ap = ap.rearrange("p (a b c) -> p a b c", a=shape[0], b=shape[1])
        self.off += nb
        return ap


def V(S, eng, method, *args, reads=(), writes=(), **kw):
    return S.add(eng, lambda e: getattr(e, method)(*args, **kw), reads, writes)


def MM(S, out, pairs, reads=(), writes=()):
    def fn(e):
        n = len(pairs)
        ins = None
        for i, (l, r) in enumerate(pairs):
            ins = e.matmul(out, l, r, start=(i == 0), stop=(i == n - 1))
        return ins
    return S.add("pe", fn, reads, writes)


def barrier(S):
    last = {}
    dmas = {}
    for op in S.ops:
        if op.dma:
            dmas.setdefault((op.eng, op.dma == 'cc'), []).append(op.idx)
        elif op.fn is not None:
            last[op.eng] = op.idx
    for eng in S.ENGS:
        op = S.add(eng, None)
        for e2, idx in last.items():
            op.deps.add(idx)
        for q, lst in dmas.items():
            for d in lst[-S.n_dma_sems:]:
                op.deps.add(d)


def final_wait(S):
    op = S.add("sp", None)
    for o in S.ops[:-1]:
        if o.dma:
            op.deps.add(o.idx)


def layer_norm_tile(S, nc, X32, XB, hx, hxb, tt, sl, ones_bf, SQ, hsq, psA, hpsA, psB, hpsB, MEAN, hmean, TMP, htmp, RSTD, hrstd,
                    XC, hxc, G, Bv, eps_ap):
    for c in range(8):
        V(S, "act", "copy", XB[:, c, sl], X32[:, c, sl], reads=[hx[c][tt]], writes=[hxb[c][tt]])
        V(S, "act", "activation", SQ[:, c, :], X32[:, c, sl], AF.Square, reads=[hx[c][tt]], writes=[hsq[c]])
    MM(S, psA[:], [(ones_bf[:], XB[:, c, sl]) for c in range(8)], reads=[hxb[c][tt] for c in range(8)], writes=[hpsA])
    MM(S, psB[:], [(ones_bf[:], SQ[:, c, :]) for c in range(8)], reads=[hsq[c] for c in range(8)], writes=[hpsB])
    V(S, "act", "copy", MEAN[:], psA[:], reads=[hpsA], writes=[hmean])
    V(S, "dve", "tensor_tensor", TMP[:], MEAN[:], MEAN[:], ALU.mult, reads=[hmean], writes=[htmp])
    V(S, "dve", "scalar_tensor_tensor", TMP[:], psB[:], LN_EPS, TMP[:], ALU.add, ALU.subtract, reads=[hpsB, htmp], writes=[htmp])
    V(S, "act", "activation", TMP[:], TMP[:], AF.Sqrt, reads=[htmp], writes=[htmp])
    V(S, "dve", "reciprocal", RSTD[:], TMP[:], reads=[htmp], writes=[hrstd])
    for c in range(8):
        j = c % 2
        V(S, "dve", "tensor_tensor", XC[j][:], X32[:, c, sl], MEAN[:], ALU.subtract, reads=[hx[c][tt], hmean], writes=[hxc[j]])
        V(S, "pool", "tensor_tensor", XC[j][:], XC[j][:], RSTD[:], ALU.mult, reads=[hxc[j], hrstd], writes=[hxc[j]])
        V(S, "act", "activation", X32[:, c, sl], XC[j][:], AF.Identity, bias=Bv[:, c:c + 1], scale=G[:, c:c + 1],
          reads=[hxc[j]], writes=[hx[c][tt]])
        V(S, "pool", "tensor_copy", XB[:, c, sl], X32[:, c, sl], reads=[hx[c][tt]], writes=[hxb[c][tt]])


GP = int(os.environ.get('GP', '9'))
ST = os.environ.get('MST', 'rot,conv,tok,cols,ret,pool,gdn,gscan,gout').split(',')

SEQ = 4096
NTILE_M = SEQ // 512
NCOL = 2308
RMS_EPS = 1e-6
GN_EPS = 1e-5
C_ID, C_TRI, C_GCROW, C_NW, C_M2, C_M1, C_ONES, C_ONES2 = 0, 128, 256, 384, 512, 640, 768, 896
C_COLS = 1024
C32_N = 1024 + 8
B_ID, B_CAUS, B_MD, B_MO, B_MD0, B_ONES = 0, 128, 256, 512, 768, 1024
CBF_N = 1152


def emit_T(C, l):
    nc, S, A, ps, hps = C.nc, C.S, C.A, C.ps, C.hps
    X32, hx, LNP, hlnp, ONESB, ONES32, hones, base = C.X32, C.hx, C.LNP, C.hlnp, C.ONESB, C.ONES32, C.hones, C.base
    D = C.D
    wg = D["wg"][l]; wbr = D["wbr"][l]; wout = D["wout"][l]; lnp = D["lnp"][l]; wr = D["wr"][l]; br = D["br"][l]
    wge = D["wge"][l]; wue = D["wue"][l]; wde = D["wde"][l]; ident = D["ident"]; sel = D["sel"]
    x32v = D["x32h"].rearrange("(c p) t -> p c t", p=128)
    xo32v = D["xo32T"].rearrange("(c p) t -> p c t", p=128)
    if True:
        A.seek(base)
        S.dma("sp", LNP, lnp, writes=[hlnp])
        WG = A.carve([8, 3072], BF16); hwg = S.T("wg")
        WBR = A.carve([12, 1024], BF16); hwbr = S.T("wbr")
        WOUT = A.carve([8, 1024], BF16); hwout = S.T("wout")
        XIN = A.carve([8, TT], BF16); hxin = S.T("xin")
        YT = A.carve([12, TT], BF16); hyt = S.T("yt")
        YA = A.carve([4, TT], BF16); hya = S.T("ya")
        YB = A.carve([4, TT], BF16); hyb = S.T("yb")
        GT = [A.carve([3, TT], BF16) for _ in range(2)]; hgt = [S.T("gt%d" % i) for i in range(2)]
        M32 = [A.carve([TT], F32) for _ in range(2)]; hm32 = [S.T("m32%d" % i) for i in range(2)]
        T1 = [A.carve([TT], F32) for _ in range(2)]; ht1 = [S.T("t1%d" % i) for i in range(2)]
        MIX = A.carve([8, TT], BF16); hmix = [S.T("mix%d" % c) for c in range(8)]
        print("phase A arena end", A.off)
        wgv = wg.rearrange("(k p) n -> p k n", p=128)
        wbrv = wbr.rearrange("(k p) n -> p k n", p=128)
        hwg = [[S.T() for _r in range(3)] for _ in range(8)]; hwbr = [S.T() for _ in range(8)]
        for dc in range(8):
            for r in range(3):
                cc_ = r * 1024 + dc * 128
                S.dma("pool", WG[:, :, cc_:cc_ + 128], wgv[:, :, cc_:cc_ + 128], writes=[hwg[dc][r]])
            S.dma("pool", WBR[:, :, dc * 128:(dc + 1) * 128], wbrv[:, :, dc * 128:(dc + 1) * 128], writes=[hwbr[dc]])
        woutv = wout.rearrange("(k p) n -> p k n", p=128)
        for k in range(8):
            S.dma("pool", WOUT[:, k, :], woutv[:, k, :], writes=[hwout])
        pi = 0
        for tt in range(NT):
            sl = slice(tt * TT, (tt + 1) * TT)
            if l == 0:
                S.dma("pool", XIN, x32v[:, :, sl], writes=[hxin])
            else:
                S.dma("sp", XIN, C.xmine_v[(l - 1) % 2][tt // 2][:, :, (tt % 2) * TT:(tt % 2 + 1) * TT], reads=[C.hxmine[(l - 1) % 2][tt // 2]], writes=[hxin])
            YTv = YT.rearrange("p (r q) t -> p r q t", r=3)
            for r in range(3):
                for (dst, hdst, half) in ((YA, hya, 0), (YB, hyb, 1)):
                    for s_ in range(2):
                        jq = half * 2 + tt // 2
                        S.dma("sp", dst[:, 2 * s_:2 * s_ + 2, :], C.gy_v[l % 2][jq][s_][:, 2 * r:2 * r + 2, (tt % 2) * TT:(tt % 2 + 1) * TT],
                              reads=[C.hgy[l % 2][jq]], writes=[hdst])
                V(S, "dve", "tensor_scalar", YA, YA, C.SELM[:, 0:1], None, ALU.mult, reads=[hya, C.hselm], writes=[hya])
                V(S, "dve", "scalar_tensor_tensor", YTv[:, r, :, :], YB, C.SELM[:, 1:2], YA, ALU.mult, ALU.add,
                  reads=[hyb, hya, C.hselm], writes=[hyt])
            for dc in range(8):
                j = dc % 2
                for r in range(3):
                    b = pi % 4; pi += 1
                    col = r * 1024 + dc * 128
                    MM(S, ps[b][:], [(WG[:, k, col:col + 128], XIN[:, k, :]) for k in range(8)],
                       reads=[hwg[dc][r], hxin], writes=[hps[b]])
                    V(S, "act", "activation", GT[j][:, r, :], ps[b][:], AF.Sigmoid, reads=[hps[b]], writes=[hgt[j]])
                for r in range(3):
                    b = pi % 4; pi += 1
                    MM(S, ps[b][:], [(WBR[:, r * 4 + k, dc * 128:(dc + 1) * 128], YT[:, r * 4 + k, :]) for k in range(4)],
                       reads=[hwbr[dc], hyt], writes=[hps[b]])
                    if r == 0:
                        V(S, "dve", "tensor_tensor", M32[j], ps[b][:], GT[j][:, r, :], ALU.mult,
                          reads=[hps[b], hgt[j]], writes=[hm32[j]])
                    elif r == 1:
                        V(S, "dve", "tensor_tensor", T1[j], ps[b][:], GT[j][:, r, :], ALU.mult,
                          reads=[hps[b], hgt[j]], writes=[ht1[j]])
                        V(S, "pool", "tensor_tensor", M32[j], M32[j], T1[j], ALU.add,
                          reads=[hm32[j], ht1[j]], writes=[hm32[j]])
                    else:
                        V(S, "dve", "tensor_tensor", T1[j], ps[b][:], GT[j][:, r, :], ALU.mult,
                          reads=[hps[b], hgt[j]], writes=[ht1[j]])
                        V(S, "pool", "tensor_tensor", MIX[:, dc, :], M32[j], T1[j], ALU.add,
                          reads=[hm32[j], ht1[j]], writes=[hmix[dc]])
            for oc in range(8):
                b = 4 + oc % 2
                MM(S, ps[b][:], [(WOUT[:, k, oc * 128:(oc + 1) * 128], MIX[:, k, :]) for k in range(8)],
                   reads=[hwout] + hmix, writes=[hps[b]])
                V(S, "dve", "scalar_tensor_tensor", X32[:, oc, sl], X32[:, oc, sl], ALPHA, ps[b][:], ALU.mult, ALU.add,
                  reads=[hx[oc][tt], hps[b]], writes=[hx[oc][tt]])
        barrier(S)
        A.seek(base)
        XB = A.carve([8, NTOK], BF16)
        hxb = [[S.T("xb%d_%d" % (c, t)) for t in range(NT)] for c in range(8)]
        WE = []
        for i in range(2):
            WE.append((A.carve([8, 512], BF16), A.carve([8, 512], BF16), A.carve([4, 1024], BF16)))
        hwe = [(S.T(), S.T(), S.T()) for i in range(2)]
        WR = A.carve([8, 20], F32); hwr = S.T()
        BR = A.carve([20], F32); hbr = S.T()
        ID32 = A.carve([128], F32); hid = S.T()
        SEL = A.carve([2048], F32); hsel = S.T()
        LOG = A.carve([16, 20], F32); hlog = S.T()
        COMB = A.carve([16, 16], F32); hcomb = S.T()
        COMBT = A.carve([NTOK], F32); hcombt = S.T()
        scratch = A.off
        SQ = A.carve([8, TT], BF16); hsq = [S.T() for c in range(8)]
        MEAN = A.carve([TT], F32); hmean = S.T()
        TMP = A.carve([TT], F32); htmp = S.T()
        RSTD = A.carve([TT], F32); hrstd = S.T()
        XC = [A.carve([TT], F32) for _ in range(2)]; hxc = [S.T(), S.T()]
        end1 = A.off
        A.seek(scratch)
        CB = [A.carve([NTOK], BF16) for _ in range(2)]; hcb = [S.T(), S.T()]
        G1 = [A.carve([TT], BF16) for _ in range(2)]; hg1 = [S.T(), S.T()]
        G2 = [A.carve([TT], BF16) for _ in range(2)]; hg2 = [S.T(), S.T()]
        H = [A.carve([4, TT], BF16) for _ in range(2)]; hh = [[S.T() for _ in range(4)] for _ in range(2)]
        A.seek(max(A.off, end1))
        R = {n: A.carve([16, 4], F32) for n in ["ohg", "ec", "fsel", "oh1", "fs2", "oh2", "we", "wa"]}
        R16 = A.carve([16, 16], F32)
        Rs = {n: A.carve([16], F32) for n in ["cmax", "sumc", "pg", "m1", "m2", "dm", "w1", "w2", "cw1", "cw2"]}
        hr = S.T("route")
        print("phase B arena end", A.off)

        S.dma("sp", WR, wr.rearrange("(k p) n -> p k n", p=128), writes=[hwr])
        S.dma("sp", BR[0:1, :], br, writes=[hbr])
        S.dma("sp", ID32, ident, writes=[hid])
        S.dma("sp", SEL[0:16, :], sel, writes=[hsel])

        def load_expert(e):
            i = e % 2
            S.dma("pool", WE[i][0], wge[e].rearrange("(k p) n -> p k n", p=128), writes=[hwe[i][0]])
            S.dma("pool", WE[i][1], wue[e].rearrange("(k p) n -> p k n", p=128), writes=[hwe[i][1]])
            S.dma("pool", WE[i][2], wde[e].rearrange("(k p) n -> p k n", p=128), writes=[hwe[i][2]])
        load_expert(0)
        G1g, B1g, G2g, B2g = LNP[:, 0:8], LNP[:, 8:16], LNP[:, 16:24], LNP[:, 24:32]
        for tt in range(NT):
            sl = slice(tt * TT, (tt + 1) * TT)
            layer_norm_tile(S, nc, X32, XB, hx, hxb, tt, sl, ONESB, SQ, hsq, ps[4], hps[4], ps[5], hps[5], MEAN, hmean, TMP, htmp,
                            RSTD, hrstd, XC, hxc, G1g, B1g, None)
        barrier(S)
        load_expert(1)
        for blk in range(16):
            tt = blk // 4
            bs = slice(blk * 128, (blk + 1) * 128)
            b = blk % 2
            pairs = [(X32[:, c, bs], WR[:, c, :]) for c in range(8)] + [(ONES32[0:1, :], BR[0:1, :])]
            MM(S, ps[b][:, 0:20], pairs, reads=[hx[c][tt] for c in range(8)] + [hwr, hbr, hones], writes=[hps[b]])
            V(S, "act", "copy", LOG[:, blk, :], ps[b][:, 0:20], reads=[hps[b]], writes=[hlog])
        LC = LOG[:, :, 0:4]
        LF = LOG[:, :, 4:20].rearrange("p b (g e) -> p b g e", g=4)
        bc = lambda ap: ap.unsqueeze(2).to_broadcast([128, 16, 4])
        rw = dict(reads=[hlog, hr], writes=[hr])
        V(S, "dve", "tensor_reduce", Rs["cmax"], LC, AX.X, ALU.max, **rw)
        V(S, "dve", "tensor_tensor", R["ohg"], LC, bc(Rs["cmax"]), ALU.is_equal, **rw)
        V(S, "dve", "tensor_tensor", R["ec"], LC, bc(Rs["cmax"]), ALU.subtract, **rw)
        V(S, "act", "activation", R["ec"], R["ec"], AF.Exp, **rw)
        V(S, "dve", "tensor_reduce", Rs["sumc"], R["ec"], AX.X, ALU.add, **rw)
        V(S, "dve", "reciprocal", Rs["pg"], Rs["sumc"], **rw)
        R16v = R16.rearrange("p b (g e) -> p b g e", g=4)
        V(S, "dve", "tensor_tensor", R16v, LF, R["ohg"].unsqueeze(3).to_broadcast([128, 16, 4, 4]), ALU.mult, **rw)
        V(S, "dve", "tensor_reduce", R["fsel"], R16.rearrange("p b (g e) -> p b e g", g=4), AX.X, ALU.add, **rw)
        V(S, "dve", "tensor_reduce", Rs["m1"], R["fsel"], AX.X, ALU.max, **rw)
        V(S, "dve", "tensor_tensor", R["oh1"], R["fsel"], bc(Rs["m1"]), ALU.is_equal, **rw)
        V(S, "dve", "scalar_tensor_tensor", R["fs2"], R["oh1"], -1e30, R["fsel"], ALU.mult, ALU.add, **rw)
        V(S, "dve", "tensor_reduce", Rs["m2"], R["fs2"], AX.X, ALU.max, **rw)
        V(S, "dve", "tensor_tensor", R["oh2"], R["fs2"], bc(Rs["m2"]), ALU.is_equal, **rw)
        V(S, "dve", "tensor_tensor", Rs["dm"], Rs["m2"], Rs["m1"], ALU.subtract, **rw)
        V(S, "act", "activation", Rs["dm"], Rs["dm"], AF.Exp, **rw)
        V(S, "dve", "tensor_scalar", Rs["w1"], Rs["dm"], 1.0, None, ALU.add, **rw)
        V(S, "dve", "reciprocal", Rs["w1"], Rs["w1"], **rw)
        V(S, "dve", "tensor_tensor", Rs["w2"], Rs["dm"], Rs["w1"], ALU.mult, **rw)
        V(S, "dve", "tensor_tensor", Rs["cw1"], Rs["w1"], Rs["pg"], ALU.mult, **rw)
        V(S, "dve", "tensor_tensor", Rs["cw2"], Rs["w2"], Rs["pg"], ALU.mult, **rw)
        V(S, "dve", "tensor_tensor", R["wa"], R["oh1"], bc(Rs["cw1"]), ALU.mult, **rw)
        V(S, "dve", "tensor_tensor", R["we"], R["oh2"], bc(Rs["cw2"]), ALU.mult, **rw)
        V(S, "dve", "tensor_tensor", R["we"], R["we"], R["wa"], ALU.add, **rw)
        COMBv = COMB.rearrange("p b (g e) -> p b g e", g=4)
        V(S, "dve", "tensor_tensor", COMBv, R["ohg"].unsqueeze(3).to_broadcast([128, 16, 4, 4]),
          R["we"].unsqueeze(2).to_broadcast([128, 16, 4, 4]), ALU.mult, reads=[hr], writes=[hcomb])
        for blk in range(16):
            b = blk % 2
            S.add("pe", lambda e, b=b, blk=blk: e.transpose(ps[b][0:16, 0:128], COMB[:, blk, :], ID32),
                  reads=[hcomb, hid], writes=[hps[b]])
            V(S, "act", "copy", COMBT[0:16, blk * 128:(blk + 1) * 128], ps[b][0:16, 0:128], reads=[hps[b]], writes=[hcombt])
        pi = 0
        for e in range(16):
            i = e % 2
            Wg, Wu, Wd = WE[i]
            for tt in range(NT):
                sl = slice(tt * TT, (tt + 1) * TT)
                MM(S, ps[4][:], [(SEL[0:16, e * 128:(e + 1) * 128], COMBT[0:16, sl])], reads=[hsel, hcombt], writes=[hps[4]])
                V(S, "act", "copy", CB[i][:, sl], ps[4][:], reads=[hps[4]], writes=[hcb[i]])
            for tt in range(NT):
                sl = slice(tt * TT, (tt + 1) * TT)
                hb = tt % 2
                for hc in range(4):
                    j = pi % 2; pi += 1
                    bg, bu = j, 2 + j
                    MM(S, ps[bg][:], [(Wg[:, k, hc * 128:(hc + 1) * 128], XB[:, k, sl]) for k in range(8)],
                       reads=[hwe[i][0]] + [hxb[k][tt] for k in range(8)], writes=[hps[bg]])
                    MM(S, ps[bu][:], [(Wu[:, k, hc * 128:(hc + 1) * 128], XB[:, k, sl]) for k in range(8)],
                       reads=[hwe[i][1]] + [hxb[k][tt] for k in range(8)], writes=[hps[bu]])
                    V(S, "act", "activation", G1[j], ps[bg][:], AF.Silu, reads=[hps[bg]], writes=[hg1[j]])
                    V(S, "pool", "tensor_tensor", G2[j], G1[j], CB[i][:, sl], ALU.mult, reads=[hg1[j], hcb[i]], writes=[hg2[j]])
                    V(S, "dve", "tensor_tensor", H[hb][:, hc, :], ps[bu][:], G2[j], ALU.mult,
                      reads=[hps[bu], hg2[j]], writes=[hh[hb][hc]])
                for oc in range(8):
                    b = 4 + oc % 2
                    MM(S, ps[b][:], [(Wd[:, hc, oc * 128:(oc + 1) * 128], H[hb][:, hc, :]) for hc in range(4)],
                       reads=[hwe[i][2]] + hh[hb], writes=[hps[b]])
                    if e == 0:
                        V(S, "dve", "scalar_tensor_tensor", X32[:, oc, sl], X32[:, oc, sl], ALPHA, ps[b][:], ALU.mult, ALU.add,
                          reads=[hx[oc][tt], hps[b]], writes=[hx[oc][tt]])
                    else:
                        V(S, "dve", "tensor_tensor", X32[:, oc, sl], X32[:, oc, sl], ps[b][:], ALU.add,
                          reads=[hx[oc][tt], hps[b]], writes=[hx[oc][tt]])
            if e + 2 < 16:
                load_expert(e + 2)
        barrier(S)
        for tt in range(NT):
            sl = slice(tt * TT, (tt + 1) * TT)
            layer_norm_tile(S, nc, X32, XB, hx, hxb, tt, sl, ONESB, SQ, hsq, ps[4], hps[4], ps[5], hps[5], MEAN, hmean, TMP, htmp,
                            RSTD, hrstd, XC, hxc, G2g, B2g, None)
            if l == NL_FUSED - 1:
                for c in range(8):
                    S.dma("sp", xo32v[:, c, sl], X32[:, c, sl], reads=[hx[c][tt]])
            else:
                S.dma("sp", C.xmine_v[l % 2][tt // 2][:, :, (tt % 2) * TT:(tt % 2 + 1) * TT], XB[:, :, sl], reads=[hxb[c][tt] for c in range(8)],
                      writes=[C.hxmine[l % 2][tt // 2]])
                if tt % 2 == 1:
                    S.add("pool", lambda e, l=l, j=tt // 2: e.collective_compute("AllGather", ALU.bypass, C.RG, [C.xmine[l % 2][j].opt()], [C.gx[l % 2][j].opt()]),
                          reads=[C.hxmine[l % 2][tt // 2]], writes=[C.hgx[l % 2][tt // 2]], dma="cc")


class Buf:
    __slots__ = ("ap", "h")

    def __init__(self, ap, h):
        self.ap = ap
        self.h = h

    def __getitem__(self, k):
        return self.ap[k]


def emit_M(C, l):
    nc, S, A = C.nc, C.S, C.A
    D = C.D
    w = D["w"][l]; tabs = D["tabs"]; convw = D["convw"][l]; c32 = D["c32"][l]; cbf = D["cbf"]; poolw = D["poolw"][l]
    xv = D["x32f"].rearrange("(c p) t -> p c t", p=128)
    wv = w.rearrange("(k p) n -> p k n", p=128)
    if True:
        A.seek(C.base)

        def sb(shape, dt, name=""):
            return Buf(A.carve(shape, dt), S.T(name))
        pbig = [Buf(C.ps[i][:], C.hps[i]) for i in range(2)]
        psm = [Buf(C.ps[2 + i][:, 0:128], C.hps[2 + i]) for i in range(4)]
        ptb = [Buf(C.ptb[i][:, 0:128], C.hptb[i]) for i in range(2)]
        cnt = {"big": 0, "sm": 0, "tb": 0}

        def PB():
            cnt["big"] += 1
            return pbig[cnt["big"] % 2]

        def PS():
            cnt["sm"] += 1
            return psm[cnt["sm"] % 4]

        def PT_():
            cnt["tb"] += 1
            return ptb[cnt["tb"] % 2]

        def op(eng, method, *args, r=(), w=(), **kw):
            return S.add(eng, lambda e: getattr(e, method)(*args, **kw), [b.h for b in r], [b.h for b in w])

        def mm(out, pairs, r=(), w=()):
            def fn(e):
                n = len(pairs)
                ins = None
                for i, (l, rr) in enumerate(pairs):
                    ins = e.matmul(out, l, rr, start=(i == 0), stop=(i == n - 1))
                return ins
            return S.add("pe", fn, [b.h for b in r], [b.h for b in w])

        def tr(out, in_, ident, r=(), w=()):
            return S.add("pe", lambda e: e.transpose(out, in_, ident), [b.h for b in r], [b.h for b in w])

        def dma(eng, out, in_, r=(), w=()):
            return S.dma(eng, out, in_, reads=[b.h for b in r], writes=[b.h for b in w])

        W = sb([8, NCOL], BF16, "W")
        C32 = sb([C32_N], F32, "C32")
        CBF = sb([CBF_N], BF16, "CBF")
        CW = sb([24], F32, "CW")
        PW = sb([2, 128], BF16, "PW")
        NEGA = sb([2], F32, "NEGA")
        SRET = sb([128], F32, "SRET"); SRETB = sb([128], BF16, "SRETB")
        SG = [sb([128], F32, "SG%d" % h) for h in range(2)]
        SGB = [sb([128], BF16, "SGB%d" % h) for h in range(2)]
        UPREV = sb([256], BF16, "UPREV")
        CIN = [sb([3 + TT], F32, "CIN%d" % g) for g in range(6)]
        Wg_ = [Buf(W.ap, S.T()) for _ in range(3)]
        wbuf = lambda col: Wg_[0 if col < 512 else (1 if col < 1280 else 2)]
        for gi_, (c0_, c1_) in enumerate(((0, 512), (512, 1280), (1280, NCOL))):
            dma("pool", W[:, :, c0_:c1_], wv[:, :, c0_:c1_], w=[Wg_[gi_]])
        dma("sp", C32.ap, c32, w=[C32])
        dma("pool", CBF.ap, cbf, w=[CBF])
        dma("sp", CW.ap, convw, w=[CW])
        dma("pool", PW.ap, poolw.rearrange("g c d -> c g d"), w=[PW])
        ID32 = C32[:, C_ID:C_ID + 128]; TRI = C32[:, C_TRI:C_TRI + 128]; GCROW = C32[:, C_GCROW:C_GCROW + 128]
        NW = C32[:, C_NW:C_NW + 128]; M2 = C32[:, C_M2:C_M2 + 128]; M1 = C32[:, C_M1:C_M1 + 128]
        ONESF = C32[:, C_ONES:C_ONES + 128]
        ONES2 = C32[:, C_ONES2:C_ONES2 + 128]
        GCCOL = C32[:, C_COLS:C_COLS + 1]; DTB = C32[:, C_COLS + 1:C_COLS + 3]; ALOG = C32[:, C_COLS + 3:C_COLS + 5]
        PSC = C32[:, C_COLS + 5:C_COLS + 7]
        ZEROC = C32[:, C_COLS + 7:C_COLS + 8]
        IDB = CBF[:, B_ID:B_ID + 128]; CAUS = CBF[:, B_CAUS:B_CAUS + 128]; ONESB = CBF[:, B_ONES:B_ONES + 128]
        op("act", "activation", NEGA.ap, ALOG, AF.Exp, r=[C32], w=[NEGA])
        op("dve", "tensor_scalar", NEGA.ap, NEGA.ap, -1.0, None, ALU.mult, r=[NEGA], w=[NEGA])
        op("dve", "memset", SRET.ap, 0.0, w=[SRET]); op("dve", "memset", SRETB.ap, 0.0, w=[SRETB])
        for h in range(2):
            op("dve", "memset", SG[h].ap, 0.0, w=[SG[h]]); op("dve", "memset", SGB[h].ap, 0.0, w=[SGB[h]])
        op("dve", "memset", UPREV.ap, 0.0, w=[UPREV])
        for g in range(6):
            op("pool", "memset", CIN[g].ap, 0.0, w=[CIN[g]])
        XT = sb([8, TT], BF16, "XT")
        TAB = sb([4, TT], F32, "TAB")
        RT1 = sb([TT], F32); RT2 = sb([TT], F32)
        QTr = sb([TT], BF16); KTr = sb([TT], BF16)
        ACC = [sb([TT], F32) for _ in range(2)]
        CO = [sb([TT], F32) for _ in range(2)]
        SQB = sb([TT], BF16); RN = sb([TT], F32)
        QH = [sb([TT], BF16) for _ in range(2)]; KH = [sb([TT], BF16) for _ in range(2)]; VTB = [sb([TT], BF16) for _ in range(2)]
        RV = sb([4, 256], BF16); RGS = sb([4, 256], F32); PU = sb([4, 256], BF16); ZS = sb([4, 256], F32)
        BA = sb([4, 4], F32)
        COLS = {n: sb([4, 2], F32, n) for n in ["beta", "g", "gc", "egc", "kb", "kd", "t"]}
        SM = sb([128], BF16); KTOK = sb([128], BF16)
        ORET = sb([8, 128], F32); MVR = sb([8, 2], F32); RSTDR = sb([8], F32)
        YTOK = sb([128], BF16)
        SQ8 = sb([8, 128], F32)
        PTt = sb([2, TT], BF16)
        YOUT = sb([6, TT], BF16)
        gdL = [{n: sb([128], BF16, n) for n in ["KBG", "KDEC", "VB", "A", "AT", "R0", "R1", "P0", "P1", "Q0", "Q1", "WT", "QDEC", "ATT", "VN"]} for _ in range(2)]
        gfL = [{n: sb([128], F32, n) for n in ["GTRI", "EGC", "E1", "E2", "U"]} for _ in range(2)]
        OG = ORET; RSTDG = sb([8], F32)
        print("M arena end", A.off)
        if len(ST) < 9:
            op("pool", "memset", YOUT.ap, 0.0, w=[YOUT])

        for tt in range(NTILE_M):
            sl = slice(tt * TT, (tt + 1) * TT)
            if l == 0:
                dma("pool", XT.ap, xv[:, :, sl], w=[XT])
            else:
                jx = (tt % 4) // 2
                S.dma("sp", XT.ap, C.gx_v[(l - 1) % 2][jx][tt // 4][:, :, (tt % 2) * TT:(tt % 2 + 1) * TT], reads=[C.hgx[(l - 1) % 2][jx]], writes=[XT.h])
            dma("sp", TAB.ap, tabs[:, :, sl].rearrange("f p t -> p f t"), w=[TAB])

            def projF(col, M=128):
                pb = PB()
                mm(pb[0:M, :], [(W[:, k, col:col + M], XT[:, k, :]) for k in range(8)], r=[wbuf(col), XT], w=[pb])
                return pb
            if 'rot' in ST:
                for (c0, ci, dst) in ((0, 0, QTr), (256, 2, KTr)):
                    p1 = projF(c0); p2 = projF(c0 + 128)
                    op("dve", "tensor_tensor", RT1.ap, p1.ap, TAB[:, ci, :], ALU.mult, r=[p1, TAB], w=[RT1])
                    op("dve", "tensor_tensor", RT2.ap, p2.ap, TAB[:, ci + 1, :], ALU.mult, r=[p2, TAB], w=[RT2])
                    op("pool", "tensor_tensor", dst.ap, RT1.ap, RT2.ap, ALU.add, r=[RT1, RT2], w=[dst])
            if 'conv' in ST:
                for g in range(6):
                    kind, h = g // 2, g % 2
                    pb = projF(512 + g * 128)
                    ci = CIN[g]
                    op("act", "copy", ci[:, 3:3 + TT], pb.ap, r=[pb], w=[ci])
                    a = ACC[g % 2]
                    eng = "dve"
                    op(eng, "tensor_scalar", a.ap, ci[:, 0:TT], CW[:, g * 4:g * 4 + 1], None, ALU.mult, r=[ci, CW], w=[a])
                    for j in range(1, 4):
                        op(eng, "scalar_tensor_tensor", a.ap, ci[:, j:j + TT], CW[:, g * 4 + j:g * 4 + j + 1], a.ap, ALU.mult, ALU.add,
                           r=[ci, CW, a], w=[a])
                    op("pool", "tensor_copy", ci[:, 0:3], ci[:, TT:TT + 3], r=[ci, a], w=[ci])
                    co = CO[g % 2]
                    op("act", "activation", co.ap, a.ap, AF.Silu, r=[a], w=[co])
                    if kind < 2:
                        op("act", "activation", SQB.ap, co.ap, AF.Square, r=[co], w=[SQB])
                        pb2 = PB()
                        mm(pb2.ap, [(ONESB, SQB.ap)], r=[CBF, SQB], w=[pb2])
                        op("dve", "tensor_scalar", RN.ap, pb2.ap, RMS_EPS, None, ALU.add, r=[pb2], w=[RN])
                        op("act", "activation", RN.ap, RN.ap, AF.Sqrt, r=[RN], w=[RN])
                        op("dve", "reciprocal", RN.ap, RN.ap, r=[RN], w=[RN])
                        dst = QH[h] if kind == 0 else KH[h]
                        sc = (128.0 ** -0.5) if kind == 0 else 1.0
                        op("dve", "scalar_tensor_tensor", dst.ap, co.ap, sc, RN.ap, ALU.mult, ALU.mult, r=[co, RN], w=[dst])
                    else:
                        op("pool", "tensor_copy", VTB[h].ap, co.ap, r=[co], w=[VTB[h]])
            if 'tok' in ST:
                for blk in range(4):
                    bs = slice(blk * 128, (blk + 1) * 128)
                    pa = PB()
                    mm(pa.ap, [(XT[:, k, bs], W[:, k, 1280:1792]) for k in range(8)], r=[wbuf(1280), XT], w=[pa])
                    op("act", "copy", RV[:, blk, :], pa[:, 0:256], r=[pa], w=[RV])
                    op("act", "activation", RGS[:, blk, :], pa[:, 256:512], AF.Silu, r=[pa], w=[RGS])
                    pb = PB()
                    mm(pb.ap, [(XT[:, k, bs], W[:, k, 1792:2304]) for k in range(8)], r=[wbuf(1280), XT], w=[pb])
                    op("act", "copy", PU[:, blk, :], pb[:, 0:256], r=[pb], w=[PU])
                    op("act", "activation", ZS[:, blk, :], pb[:, 256:512], AF.Silu, r=[pb], w=[ZS])
                    pc = PS()
                    mm(pc[:, 0:4], [(XT[:, k, bs], W[:, k, 2304:2308]) for k in range(8)], r=[wbuf(1280), XT], w=[pc])
                    op("dve", "tensor_copy", BA[:, blk, :], pc[:, 0:4], r=[pc], w=[BA])
            if 'cols' in ST:
                cb = COLS
                rw = lambda *bs_: dict(r=list(bs_), w=[bs_[-1]])
                op("act", "activation", cb["beta"].ap, BA[:, :, 0:2], AF.Exp, scale=-1.0, r=[BA], w=[cb["beta"]])
                op("dve", "tensor_scalar", cb["beta"].ap, cb["beta"].ap, 1.0, None, ALU.add, r=[cb["beta"]], w=[cb["beta"]])
                op("dve", "reciprocal", cb["beta"].ap, cb["beta"].ap, r=[cb["beta"]], w=[cb["beta"]])
                op("dve", "tensor_tensor", cb["g"].ap, BA[:, :, 2:4], DTB.unsqueeze(1).to_broadcast([128, 4, 2]), ALU.add, r=[BA, C32], w=[cb["g"]])
                op("act", "activation", cb["g"].ap, cb["g"].ap, AF.Exp, r=[cb["g"]], w=[cb["g"]])
                op("dve", "tensor_scalar", cb["g"].ap, cb["g"].ap, 1.0, None, ALU.add, r=[cb["g"]], w=[cb["g"]])
                op("act", "activation", cb["g"].ap, cb["g"].ap, AF.Ln, r=[cb["g"]], w=[cb["g"]])
                op("dve", "tensor_tensor", cb["g"].ap, cb["g"].ap, NEGA.ap.unsqueeze(1).to_broadcast([128, 4, 2]), ALU.mult, r=[cb["g"], NEGA], w=[cb["g"]])
                pg = PS()
                mm(pg[:, 0:8], [(TRI, cb["g"].ap.rearrange("p b h -> p (b h)"))], r=[C32, cb["g"]], w=[pg])
                op("dve", "tensor_copy", cb["gc"].ap.rearrange("p b h -> p (b h)"), pg[:, 0:8], r=[pg], w=[cb["gc"]])
                op("act", "activation", cb["egc"].ap, cb["gc"].ap, AF.Exp, r=[cb["gc"]], w=[cb["egc"]])
                op("dve", "tensor_tensor", cb["kb"].ap, cb["egc"].ap, cb["beta"].ap, ALU.mult, r=[cb["egc"], cb["beta"]], w=[cb["kb"]])
                op("dve", "tensor_scalar", cb["t"].ap, cb["gc"].ap, -1.0, None, ALU.mult, r=[cb["gc"]], w=[cb["t"]])
                pgl = PS()
                mm(pgl[:, 0:8], [(ONES2, cb["g"].ap.rearrange("p b h -> p (b h)"))], r=[C32, cb["g"]], w=[pgl])
                op("dve", "tensor_tensor", cb["kd"].ap.rearrange("p b h -> p (b h)"), pgl[:, 0:8], cb["gc"].ap.rearrange("p b h -> p (b h)"), ALU.subtract,
                   r=[pgl, cb["gc"]], w=[cb["kd"]])
                op("act", "activation", cb["kd"].ap, cb["kd"].ap, AF.Exp, r=[cb["kd"]], w=[cb["kd"]])

            if 'ret' in ST:
                for blk in range(4):
                    bs = slice(blk * 128, (blk + 1) * 128)
                    pt = PT_()
                    tr(pt.ap, KTr[:, bs], IDB, r=[KTr, CBF], w=[pt])
                    op("dve", "tensor_tensor", KTOK.ap, pt.ap, GCROW, ALU.mult, r=[pt, C32], w=[KTOK])
                    pkvb = PB(); pkv = Buf(pkvb.ap[:, 0:128], pkvb.h)
                    for h in range(2):
                        hs = slice(h * 64, (h + 1) * 64)
                        psc = PS()
                        mm(psc.ap, [(KTr[hs, bs], QTr[hs, bs])], r=[KTr, QTr], w=[psc])
                        op("dve", "tensor_tensor", SM.ap, psc.ap, CAUS, ALU.mult, r=[psc, CBF], w=[SM])
                        po = PS()
                        mm(po.ap, [(SM.ap, RV[:, blk, h * 128:(h + 1) * 128]), (QTr[hs, bs], SRETB[hs, :])], r=[SM, RV, QTr, SRETB], w=[po])
                        idx = blk * 2 + h
                        op("act", "copy", ORET[:, idx, :], po.ap, r=[po], w=[ORET])
                        mm(pkv[hs, :], [(KTOK[:, hs], RV[:, blk, h * 128:(h + 1) * 128])], r=[KTOK, RV], w=[pkv])
                    op("dve", "scalar_tensor_tensor", SRET.ap, SRET.ap, GCCOL, pkv.ap, ALU.mult, ALU.add, r=[SRET, C32, pkv], w=[SRET])
                    op("act", "copy", SRETB.ap, SRET.ap, r=[SRET], w=[SRETB])
                op("dve", "tensor_reduce", MVR[:, :, 0], ORET.ap, AX.X, ALU.add, r=[ORET], w=[MVR])
                op("act", "activation", SQ8.ap, ORET.ap, AF.Square, r=[ORET], w=[SQ8])
                op("dve", "tensor_reduce", MVR[:, :, 1], SQ8.ap, AX.X, ALU.add, r=[SQ8, MVR], w=[MVR])
                op("dve", "tensor_scalar", MVR.ap, MVR.ap, 1.0 / 128.0, None, ALU.mult, r=[MVR], w=[MVR])
                op("dve", "tensor_tensor", RSTDR.ap, MVR[:, :, 0], MVR[:, :, 0], ALU.mult, r=[MVR], w=[RSTDR])
                op("dve", "tensor_tensor", RSTDR.ap, MVR[:, :, 1], RSTDR.ap, ALU.subtract, r=[MVR, RSTDR], w=[RSTDR])
                op("dve", "tensor_scalar", RSTDR.ap, RSTDR.ap, GN_EPS, None, ALU.add, r=[RSTDR], w=[RSTDR])
                op("act", "activation", RSTDR.ap, RSTDR.ap, AF.Sqrt, r=[RSTDR], w=[RSTDR])
                op("dve", "reciprocal", RSTDR.ap, RSTDR.ap, r=[RSTDR], w=[RSTDR])
                for blk in range(4):
                    for h in range(2):
                        idx = blk * 2 + h
                        op("dve", "tensor_scalar", ORET[:, idx, :], ORET[:, idx, :], MVR[:, idx, 0:1], RSTDR[:, idx:idx + 1], ALU.subtract, ALU.mult,
                           r=[ORET, MVR, RSTDR], w=[ORET])
                        op("pool", "tensor_tensor", YTOK.ap, ORET[:, idx, :], RGS[:, blk, h * 128:(h + 1) * 128], ALU.mult, r=[ORET, RGS], w=[YTOK])
                        pt = PT_()
                        tr(pt.ap, YTOK.ap, IDB, r=[YTOK, CBF], w=[pt])
                        op("act", "copy", YOUT[:, h, blk * 128:(blk + 1) * 128], pt.ap, r=[pt], w=[YOUT])
            if 'pool' in ST:
                for g in range(2):
                    for blk in range(4):
                        pp = PS()
                        ug = PU[:, blk, g * 128:(g + 1) * 128]
                        if tt == 0 and blk == 0:
                            mm(pp.ap, [(ug, CBF[:, B_MD0 + g * 128:B_MD0 + (g + 1) * 128])], r=[PU, CBF], w=[pp])
                        else:
                            uprev = UPREV[:, g * 128:(g + 1) * 128] if blk == 0 else PU[:, blk - 1, g * 128:(g + 1) * 128]
                            mm(pp.ap, [(ug, CBF[:, B_MD + g * 128:B_MD + (g + 1) * 128]), (uprev, CBF[:, B_MO + g * 128:B_MO + (g + 1) * 128])],
                               r=[PU, UPREV, CBF], w=[pp])
                        op("act", "copy", PTt[:, g, blk * 128:(blk + 1) * 128], pp.ap, r=[pp], w=[PTt])
                    MP = int(os.environ.get("MPOOL", "4"))
                    if MP >= 2:
                        pb = PB()
                        mm(pb.ap, [(PW[:, g, :], PTt[:, g, :])], r=[PW, PTt], w=[pb])
                    if MP >= 3:
                        op("act", "activation", YOUT[:, 2 + g, :], pb.ap, AF.Identity, scale=PSC[:, g:g + 1], r=[pb, C32], w=[YOUT])
                if MP >= 4:
                    op("pool", "tensor_copy", UPREV.ap, PU[:, 3, :], r=[PU], w=[UPREV])
            if 'gdn' in ST:
                for blk in range(4):
                    bs = slice(blk * 128, (blk + 1) * 128)
                    def gdn_block(blk, h, bs):
                        gd = gdL[h]; gf = gfL[h]
                        gcol = cb["g"][:, blk, h:h + 1]; gccol = cb["gc"][:, blk, h:h + 1]
                        betac = cb["beta"][:, blk, h:h + 1]; kbc = cb["kb"][:, blk, h:h + 1]
                        yield
                        op("dve", "tensor_scalar", gf["GTRI"].ap, TRI, gcol, None, ALU.mult, r=[C32, cb["g"]], w=[gf["GTRI"]])
                        pgc = PS()
                        yield
                        mm(pgc.ap, [(ONESF, gf["GTRI"].ap)], r=[C32, gf["GTRI"]], w=[pgc])
                        yield
                        op("act", "activation", gf["EGC"].ap, pgc.ap, AF.Exp, r=[pgc], w=[gf["EGC"]])
                        yield
                        op("act", "activation", gf["E1"].ap, pgc.ap, AF.Exp, bias=cb["t"][:, blk, h:h + 1], r=[pgc, cb["t"]], w=[gf["E1"]])
                        yield
                        op("act", "activation", gf["E2"].ap, pgc.ap, AF.Exp, bias=gccol, scale=-1.0, r=[pgc, cb["gc"]], w=[gf["E2"]])
                        yield
                        op("dve", "scalar_tensor_tensor", gf["E1"].ap, gf["E1"].ap, 1.0, M1, ALU.min, ALU.mult, r=[gf["E1"], C32], w=[gf["E1"]])
                        yield
                        op("dve", "scalar_tensor_tensor", gf["E2"].ap, gf["E2"].ap, 1.0, M2, ALU.min, ALU.mult, r=[gf["E2"], C32], w=[gf["E2"]])
                        if GP <= 1: return
                        ptk = PT_()
                        yield
                        tr(ptk.ap, KH[h][:, bs], IDB, r=[KH[h], CBF], w=[ptk])
                        yield
                        op("act", "activation", gd["KBG"].ap, ptk.ap, AF.Identity, scale=kbc, r=[ptk, cb["kb"]], w=[gd["KBG"]])
                        yield
                        op("act", "activation", gd["KDEC"].ap, ptk.ap, AF.Identity, scale=cb["kd"][:, blk, h:h + 1], r=[ptk, cb["kd"]], w=[gd["KDEC"]])
                        ptv = PT_()
                        yield
                        tr(ptv.ap, VTB[h][:, bs], IDB, r=[VTB[h], CBF], w=[ptv])
                        yield
                        op("act", "activation", gd["VB"].ap, ptv.ap, AF.Identity, scale=betac, r=[ptv, cb["beta"]], w=[gd["VB"]])
                        if GP <= 2: return
                        pkk = PS()
                        yield
                        mm(pkk.ap, [(KH[h][:, bs], KH[h][:, bs])], r=[KH[h]], w=[pkk])
                        yield
                        op("dve", "scalar_tensor_tensor", gd["A"].ap, pkk.ap, betac, gf["E2"].ap, ALU.mult, ALU.mult, r=[pkk, cb["beta"], gf["E2"]], w=[gd["A"]])
                        pqk = PS()
                        yield
                        mm(pqk.ap, [(KH[h][:, bs], QH[h][:, bs])], r=[KH[h], QH[h]], w=[pqk])
                        yield
                        op("dve", "tensor_tensor", gd["ATT"].ap, pqk.ap, gf["E1"].ap, ALU.mult, r=[pqk, gf["E1"]], w=[gd["ATT"]])
                        yield
                        op("dve", "tensor_tensor", gd["QDEC"].ap, QH[h][:, bs], gf["EGC"].ap, ALU.mult, r=[QH[h], gf["EGC"]], w=[gd["QDEC"]])
                        if GP <= 3: return
                        pat = PT_()
                        yield
                        tr(pat.ap, gd["A"].ap, IDB, r=[gd["A"], CBF], w=[pat])
                        yield
                        op("dve", "tensor_copy", gd["AT"].ap, pat.ap, r=[pat], w=[gd["AT"]])
                        yield
                        op("dve", "scalar_tensor_tensor", gd["R0"].ap, pat.ap, -1.0, IDB, ALU.mult, ALU.add, r=[CBF, pat], w=[gd["R0"]])
                        if GP <= 4: return
                        Pc, Qc, Rc = gd["A"], gd["AT"], gd["R0"]
                        Pn = [gd["P0"], gd["P1"]]; Qn = [gd["Q0"], gd["Q1"]]; Rn = [gd["R1"], gd["R0"]]
                        for k in range(1, 6):
                            pp_ = PS()
                            yield
                            mm(pp_.ap, [(Qc.ap, Pc.ap)], r=[Qc, Pc], w=[pp_])
                            Pnew = Pn[k % 2]
                            yield
                            op("act", "copy", Pnew.ap, pp_.ap, r=[pp_], w=[Pnew])
                            if k < 5:
                                pq_ = PS()
                                mm(pq_.ap, [(Pc.ap, Qc.ap)], r=[Qc, Pc], w=[pq_])
                                Qnew = Qn[k % 2]
                                op("dve", "tensor_copy", Qnew.ap, pq_.ap, r=[pq_], w=[Qnew])
                            pr_ = PS()
                            yield
                            mm(pr_.ap, [(IDB, Rc.ap), (Pnew.ap, Rc.ap)], r=[CBF, Rc, Pnew], w=[pr_])
                            Rnew = Rn[(k - 1) % 2]
                            yield
                            op("dve", "tensor_copy", Rnew.ap, pr_.ap, r=[pr_], w=[Rnew])
                            Pc, Rc = Pnew, Rnew
                            if k < 5:
                                Qc = Qnew
                        if GP <= 5: return
                        TTm = Rc
                        pw = PS()
                        yield
                        mm(pw.ap, [(gd["KBG"].ap, TTm.ap)], r=[gd["KBG"], TTm], w=[pw])
                        yield
                        op("act", "copy", gd["WT"].ap, pw.ap, r=[pw], w=[gd["WT"]])
                        pu_ = PS()
                        yield
                        mm(pu_.ap, [(TTm.ap, gd["VB"].ap)], r=[TTm, gd["VB"]], w=[pu_])
                        yield
                        op("act", "copy", gf["U"].ap, pu_.ap, r=[pu_], w=[gf["U"]])
                        if GP <= 6: return
                        for c in (range(2) if 'gscan' in ST else []):
                            rs = slice(c * 64, (c + 1) * 64)
                            pws = PS()
                            yield
                            mm(pws[rs, :], [(gd["WT"][:, rs], SGB[h].ap)], r=[gd["WT"], SGB[h]], w=[pws])
                            yield
                            op("dve", "scalar_tensor_tensor", gd["VN"][rs, :], pws[rs, :], -1.0, gf["U"][rs, :], ALU.mult, ALU.add, r=[gf["U"], pws, gd["VN"]], w=[gd["VN"]])
                            pog = PS()
                            yield
                            mm(pog[rs, :], [(gd["QDEC"][:, rs], SGB[h].ap), (gd["ATT"][rs, rs], gd["VN"][rs, :])],
                               r=[gd["QDEC"], SGB[h], gd["ATT"], gd["VN"]], w=[pog])
                            idx = blk * 2 + h
                            yield
                            op("act", "copy", OG[rs, idx, :], pog[rs, :], r=[pog, OG], w=[OG])
                            pds = PS()
                            yield
                            mm(pds.ap, [(gd["KDEC"][rs, :], gd["VN"][rs, :])], r=[gd["KDEC"], gd["VN"]], w=[pds])
                            yield
                            op("dve", "scalar_tensor_tensor", SG[h].ap, SG[h].ap, gf["EGC"][:, c * 64 + 63:c * 64 + 64], pds.ap, ALU.mult, ALU.add,
                               r=[SG[h], gf["EGC"], pds], w=[SG[h]])
                            yield
                            op("act", "copy", SGB[h].ap, SG[h].ap, r=[SG[h]], w=[SGB[h]])
                    gens = [gdn_block(blk, h, bs) for h in range(2)]
                    while gens:
                        for g_ in list(gens):
                            try:
                                next(g_)
                            except StopIteration:
                                gens.remove(g_)
            if 'gout' in ST:
                op("act", "activation", SQ8.ap, OG.ap, AF.Square, r=[OG], w=[SQ8])
                op("dve", "tensor_reduce", RSTDG.ap, SQ8.ap, AX.X, ALU.add, r=[SQ8], w=[RSTDG])
                op("dve", "tensor_scalar", RSTDG.ap, RSTDG.ap, 1.0 / 128.0, None, ALU.mult, r=[RSTDG], w=[RSTDG])
                op("dve", "tensor_scalar", RSTDG.ap, RSTDG.ap, RMS_EPS, None, ALU.add, r=[RSTDG], w=[RSTDG])
                op("act", "activation", RSTDG.ap, RSTDG.ap, AF.Sqrt, r=[RSTDG], w=[RSTDG])
                op("dve", "reciprocal", RSTDG.ap, RSTDG.ap, r=[RSTDG], w=[RSTDG])
                for blk in range(4):
                    for h in range(2):
                        idx = blk * 2 + h
                        zs = ZS[:, blk, h * 128:(h + 1) * 128]
                        op("pool", "tensor_tensor", zs, zs, NW, ALU.mult, r=[ZS, C32], w=[ZS])
                        op("dve", "scalar_tensor_tensor", YTOK.ap, OG[:, idx, :], RSTDG[:, idx:idx + 1], zs, ALU.mult, ALU.mult,
                           r=[OG, RSTDG, ZS], w=[YTOK])
                        pt = PT_()
                        tr(pt.ap, YTOK.ap, IDB, r=[YTOK, CBF], w=[pt])
                        op("act", "copy", YOUT[:, 4 + h, blk * 128:(blk + 1) * 128], pt.ap, r=[pt], w=[YOUT])
            S.dma("sp", C.ymine_v[l % 2][tt // 2][:, :, (tt % 2) * TT:(tt % 2 + 1) * TT], YOUT.ap, reads=[YOUT.h], writes=[C.hymine[l % 2][tt // 2]])
            if tt % 2 == 1:
                S.add("pool", lambda e, l=l, j=tt // 2: e.collective_compute("AllGather", ALU.bypass, C.RG, [C.ymine[l % 2][j].opt()], [C.gy[l % 2][j].opt()]),
                      reads=[C.hymine[l % 2][tt // 2]], writes=[C.hgy[l % 2][tt // 2]], dma="cc")


bf16 = ml_dtypes.bfloat16

def m_consts(hh):
    t = np.arange(4096)
    half = 32
    inv_freq = (np.float32(10000.0) ** (-(np.arange(half, dtype=np.float32)) / np.float32(half))).astype(np.float32)
    ang = (t.astype(np.float32)[:, None] * inv_freq[None, :]).astype(np.float32)
    cos = np.cos(ang).astype(np.float32).T; sin = np.sin(ang).astype(np.float32).T
    lg = np.log(1.0 - np.exp2(-5.0 - np.arange(4, dtype=np.float64)))
    j = (t % 128).astype(np.float64)
    tabs = np.zeros((4, 128, 4096), np.float32)
    gcrow = np.zeros((128, 128), np.float32); gccol = np.zeros((128, 1), np.float32)
    for h in range(2):
        H = 2 * hh + h
        xi = np.exp((j + 1) * lg[H]); kf = np.exp(-(j + 1) * lg[H]) * 64 ** -0.5
        r = slice(h * 64, h * 64 + 32); r2 = slice(h * 64 + 32, h * 64 + 64)
        tabs[0, r] = cos * xi; tabs[0, r2] = cos * xi
        tabs[1, r] = -sin * xi; tabs[1, r2] = sin * xi
        tabs[2, r] = cos * kf; tabs[2, r2] = cos * kf
        tabs[3, r] = -sin * kf; tabs[3, r2] = sin * kf
        gcrow[:, h * 64:(h + 1) * 64] = np.exp(128 * lg[H]); gccol[h * 64:(h + 1) * 64, 0] = np.exp(128 * lg[H])
    p = np.arange(128)[:, None]; f = np.arange(128)[None, :]
    same = (p // 64) == (f // 64)
    tri = (same & (p <= f)).astype(np.float32)
    m1 = (same & (f >= p)).astype(np.float32)
    m2 = (same & (p > f)).astype(np.float32)
    caus = (f >= p).astype(np.float32)
    md = np.zeros((2, 128, 128), np.float32); mo = np.zeros((2, 128, 128), np.float32); md0 = np.zeros((2, 128, 128), np.float32)
    for g in range(2):
        wdw = (2, 4, 8, 16)[2 * hh + g]
        for tq in range(128):
            for s_ in range(tq - wdw + 1, tq + 1):
                if s_ >= 0: md[g, s_, tq] += 1.0 / wdw
                else: mo[g, 128 + s_, tq] += 1.0 / wdw
            cntq = min(tq + 1, wdw)
            for s_ in range(max(0, tq - wdw + 1), tq + 1):
                md0[g, s_, tq] += 1.0 / cntq
            md[g, tq, tq] -= 1.0; md0[g, tq, tq] -= 1.0
    return dict(tabs=tabs, gcrow=gcrow, gccol=gccol, tri=tri, m1=m1, m2=m2, caus=caus, md=md, mo=mo, md0=md0, same=same.astype(np.float32))

def m_inputs(inp, l, b_x32T, hh, K):
    wi = inp["w_in"][l]
    hs = [2 * hh, 2 * hh + 1]
    cols = []
    rq = lambda base, h: np.arange(base + h * 64, base + (h + 1) * 64)
    sw = lambda a: np.concatenate([a[32:], a[:32]])
    cols += [np.concatenate([rq(0, h) for h in hs]), np.concatenate([sw(rq(0, h)) for h in hs])]
    cols += [np.concatenate([rq(256, h) for h in hs]), np.concatenate([sw(rq(256, h)) for h in hs])]
    for base in (2048, 2560, 3072):
        for h in hs: cols.append(np.arange(base + h * 128, base + (h + 1) * 128))
    for base in (512, 1024, 1536, 3584):
        cols.append(np.arange(base + hs[0] * 128, base + (hs[1] + 1) * 128))
    cols.append(np.array([4096 + hs[0], 4096 + hs[1], 4100 + hs[0], 4100 + hs[1]]))
    cols = np.concatenate(cols); assert len(cols) == 2308
    m = {"x32T": np.ascontiguousarray(b_x32T), "w": np.ascontiguousarray(wi[:, cols]), "tabs": K["tabs"]}
    cw = np.zeros((128, 24), np.float32)
    for g in range(6):
        kind, h = g // 2, g % 2
        base = kind * 512 + hs[h] * 128
        cw[:, g * 4:(g + 1) * 4] = inp["conv_w"][l][:, base:base + 128].T
    m["convw"] = cw
    c32 = np.zeros((128, C32_N), np.float32)
    c32[:, C_ID:C_ID + 128] = np.eye(128); c32[:, C_TRI:C_TRI + 128] = K["tri"]; c32[:, C_GCROW:C_GCROW + 128] = K["gcrow"]
    c32[:, C_NW:C_NW + 128] = inp["gdn_norm_w"][l][None, :]; c32[:, C_M2:C_M2 + 128] = K["m2"]; c32[:, C_M1:C_M1 + 128] = K["m1"]
    c32[:, C_ONES:C_ONES + 128] = 1.0; c32[:, C_ONES2:C_ONES2 + 128] = K["same"]
    c32[:, C_COLS] = K["gccol"][:, 0]
    for h in range(2):
        c32[:, C_COLS + 1 + h] = inp["dt_bias"][l][hs[h]]; c32[:, C_COLS + 3 + h] = inp["A_log"][l][hs[h]]
        c32[:, C_COLS + 5 + h] = inp["pool_scale"][l][hs[h] * 128:(hs[h] + 1) * 128]
    m["c32"] = c32
    cb = np.zeros((128, CBF_N), np.float32)
    cb[:, B_ID:B_ID + 128] = np.eye(128); cb[:, B_CAUS:B_CAUS + 128] = K["caus"]
    for g in range(2):
        cb[:, B_MD + g * 128:B_MD + (g + 1) * 128] = K["md"][g]; cb[:, B_MO + g * 128:B_MO + (g + 1) * 128] = K["mo"][g]
        cb[:, B_MD0 + g * 128:B_MD0 + (g + 1) * 128] = K["md0"][g]
    cb[:, B_ONES:B_ONES + 128] = 1.0
    m["cbf"] = cb
    m["poolw"] = np.ascontiguousarray(inp["pool_w"][l][hs[0]:hs[1] + 1])
    return m

def t_inputs(inp, l, x32T_half, yT_half):
    IN = 7176 - 3072
    m = {}
    m["x32T"] = np.ascontiguousarray(x32T_half, dtype=np.float32)
    m["yT"] = np.ascontiguousarray(yT_half).astype(bf16) if yT_half.dtype != bf16 else np.ascontiguousarray(yT_half)
    m["wg"] = np.ascontiguousarray(inp["w_in"][l][:, IN:])
    m["wbr"] = np.ascontiguousarray(inp["w_branch"][l].reshape(1536, 1024))
    m["wout"] = np.ascontiguousarray(inp["w_out"][l])
    f = lambda v: v.reshape(8, 128).T
    m["lnp"] = np.ascontiguousarray(np.concatenate([f(inp["ln1_g"][l]), f(inp["ln1_b"][l]), f(inp["ln2_g"][l]), f(inp["ln2_b"][l])], axis=1))
    m["wr"] = np.ascontiguousarray(np.concatenate([inp["router_coarse_w"][l], inp["router_fine_w"][l]], axis=1))
    m["br"] = np.ascontiguousarray(np.concatenate([inp["router_coarse_b"][l], inp["router_fine_b"][l]])[None, :])
    m["wge"] = inp["w_gate"][l]; m["wue"] = inp["w_up"][l]; m["wde"] = inp["w_down"][l]
    m["ident"] = np.eye(128, dtype=np.float32)
    sel = np.zeros((16, 16, 128), np.float32)
    for e in range(16): sel[e, e, :] = 1.0
    m["sel"] = sel.reshape(16, 2048)
    return m


class Ctx:
    pass


NL_FUSED = int(os.environ.get('FUSED_NL', '4'))
NC_FUSED = int(os.environ.get('FUSED_NC', '8'))


def build_fused():
    nc = bass.Bass("TRN2", target_bir_lowering=False)
    dr = lambda name, shape, dt, kind="ExternalInput": nc.dram_tensor(name, shape, dt, kind=kind).ap()
    D = {}
    D["x32f"] = dr("x32f", [1024, SEQ], F32)
    D["x32h"] = dr("x32h", [1024, NTOK], F32)
    D["selm"] = dr("selm", [128, 2], F32)
    D["w"] = dr("w", [4, 1024, NCOL], F32)
    D["tabs"] = dr("tabs", [4, 128, SEQ], F32)
    D["convw"] = dr("convw", [4, 128, 24], F32)
    D["c32"] = dr("c32", [4, 128, C32_N], F32)
    D["cbf"] = dr("cbf", [128, CBF_N], F32)
    D["poolw"] = dr("poolw", [4, 2, 128, 128], F32)
    D["wg"] = dr("wg", [4, 1024, 3072], F32)
    D["wbr"] = dr("wbr", [4, 1536, 1024], F32)
    D["wout"] = dr("wout", [4, 1024, 1024], F32)
    D["lnp"] = dr("lnp", [4, 128, 32], F32)
    D["wr"] = dr("wr", [4, 1024, 20], F32)
    D["br"] = dr("br", [4, 1, 20], F32)
    D["wge"] = [dr("wge%d" % l, [16, 1024, 512], F32) for l in range(4)]
    D["wue"] = [dr("wue%d" % l, [16, 1024, 512], F32) for l in range(4)]
    D["wde"] = [dr("wde%d" % l, [16, 512, 1024], F32) for l in range(4)]
    D["ident"] = dr("ident", [128, 128], F32)
    D["sel"] = dr("sel", [16, 2048], F32)
    D["xo32T"] = dr("xo32T", [1024, NTOK], F32, "ExternalOutput")
    C = Ctx()
    C.nc = nc; C.D = D
    ymine = [[nc.dram_tensor("ymine%d_%d" % (i, j), [768, 1024], BF16).ap() for j in range(4)] for i in range(2)]
    gy = [[nc.dram_tensor("gy%d_%d" % (i, j), [2 * 768, 1024], BF16).ap() for j in range(4)] for i in range(2)]
    xmine = [[nc.dram_tensor("xmine%d_%d" % (i, j), [1024, 1024], BF16).ap() for j in range(2)] for i in range(2)]
    gx = [[nc.dram_tensor("gx%d_%d" % (i, j), [2 * 1024, 1024], BF16).ap() for j in range(2)] for i in range(2)]
    pv = lambda t: t.rearrange("(c p) t -> p c t", p=128)
    C.ymine_v = [[pv(t) for t in row] for row in ymine]
    C.gy_v = [[[pv(t[s_ * 768:(s_ + 1) * 768, :]) for s_ in range(2)] for t in row] for row in gy]
    C.xmine_v = [[pv(t) for t in row] for row in xmine]
    C.gx_v = [[[pv(t[s_ * 1024:(s_ + 1) * 1024, :]) for s_ in range(2)] for t in row] for row in gx]
    RG = [[2 * i, 2 * i + 1] for i in range(NC_FUSED // 2)]
    C.RG = RG; C.ymine = ymine; C.gy = gy; C.xmine = xmine; C.gx = gx
    S = Sched(nc)
    C.S = S
    C.hymine = [[S.T() for j in range(4)] for i in range(2)]; C.hgy = [[S.T() for j in range(4)] for i in range(2)]
    C.hxmine = [[S.T() for j in range(2)] for i in range(2)]; C.hgx = [[S.T() for j in range(2)] for i in range(2)]
    with ExitStack() as es:
        A = Arena(nc, es, 204 * 1024)
        C.A = A
        C.ps = [es.enter_context(nc.psum_tensor("ps%d" % i, [128, 512], F32)) for i in range(6)]
        C.hps = [S.T("ps%d" % i) for i in range(6)]
        C.ptb = [es.enter_context(nc.psum_tensor("ptb%d" % i, [128, 1024], BF16)) for i in range(2)]
        C.hptb = [S.T("ptb%d" % i) for i in range(2)]
        C.X32 = A.carve([8, NTOK], F32)
        C.hx = [[S.T("x%d_%d" % (c, t)) for t in range(NT)] for c in range(8)]
        C.LNP = A.carve([32], F32); C.hlnp = S.T("lnp")
        C.ONESB = A.carve([128], BF16); C.hones = S.T("ones")
        C.ONES32 = A.carve([128], F32)
        C.SELM = A.carve([2], F32); C.hselm = S.T("selm")
        C.base = A.off
        V(S, "dve", "memset", C.ONESB, 1.0 / 1024.0, writes=[C.hones])
        V(S, "dve", "memset", C.ONES32, 1.0, writes=[C.hones])
        S.dma("sp", C.SELM, D["selm"], writes=[C.hselm])
        x32v = D["x32h"].rearrange("(c p) t -> p c t", p=128)
        for c in range(8):
            for t in range(NT):
                S.dma("sp", C.X32[:, c, t * TT:(t + 1) * TT], x32v[:, c, t * TT:(t + 1) * TT], writes=[C.hx[c][t]])
        for l in range(NL_FUSED):
            emit_M(C, l)
            barrier(S)
            emit_T(C, l)
            barrier(S)
        final_wait(S)
        S.emit()
    print("fused stats", S.stats)
    return nc


def kernel(**inputs):
    inp = {k: np.asarray(v) for k, v in inputs.items()}
    x = inp["x"]
    B = x.shape[0]
    xT = [np.ascontiguousarray(x[b].T).astype(np.float32) for b in range(B)]
    K = [m_consts(hh) for hh in range(2)]
    maps = []
    shared = {}
    for c in range(NC_FUSED):
        b, hh = c // 2, c % 2
        m = {}
        ml = [m_inputs(inp, l, xT[b], hh, K[hh]) for l in range(4)]
        m["x32f"] = xT[b]
        m["x32h"] = np.ascontiguousarray(xT[b][:, hh * 2048:(hh + 1) * 2048])
        sm = np.zeros((128, 2), np.float32); sm[:, hh] = 1.0
        m["selm"] = sm
        for k_ in ("w", "convw", "c32", "poolw"):
            m[k_] = np.stack([ml[l][k_] for l in range(4)], axis=0)
        m["tabs"] = ml[0]["tabs"]; m["cbf"] = ml[0]["cbf"]
        if not shared:
            IN = 7176 - 3072
            f = lambda v: v.reshape(8, 128).T
            shared["wg"] = np.ascontiguousarray(inp["w_in"][:, :, IN:])
            shared["wbr"] = np.ascontiguousarray(inp["w_branch"].reshape(4, 1536, 1024))
            shared["wout"] = np.ascontiguousarray(inp["w_out"])
            shared["lnp"] = np.stack([np.concatenate([f(inp["ln1_g"][l]), f(inp["ln1_b"][l]), f(inp["ln2_g"][l]), f(inp["ln2_b"][l])], axis=1) for l in range(4)], 0).astype(np.float32)
            shared["wr"] = np.ascontiguousarray(np.concatenate([inp["router_coarse_w"], inp["router_fine_w"]], axis=2))
            shared["br"] = np.ascontiguousarray(np.concatenate([inp["router_coarse_b"], inp["router_fine_b"]], axis=1)[:, None, :])
            for l in range(4):
                shared["wge%d" % l] = np.ascontiguousarray(inp["w_gate"][l]); shared["wue%d" % l] = np.ascontiguousarray(inp["w_up"][l])
                shared["wde%d" % l] = np.ascontiguousarray(inp["w_down"][l])
            shared["ident"] = np.eye(128, dtype=np.float32)
            sel = np.zeros((16, 16, 128), np.float32)
            for e in range(16):
                sel[e, e, :] = 1.0
            shared["sel"] = sel.reshape(16, 2048)
        m.update(shared)
        maps.append(m)
    nc = build_fused()
    res = run_bass_kernel_spmd(nc, maps, core_ids=list(range(NC_FUSED)))
    out = np.zeros((B, 4096, 1024), np.float32)
    for c in range(NC_FUSED):
        b, hh = c // 2, c % 2
        out[b, hh * 2048:(hh + 1) * 2048, :] = np.asarray(res.results[c]["xo32T"]).T
    return out
```

```python
import os, math
import numpy as np
import ml_dtypes
import concourse.bass as bass
import concourse.mybir as mybir
from concourse.bass_utils import run_bass_kernel_spmd
from contextlib import ExitStack

F32 = mybir.dt.float32
BF16 = mybir.dt.bfloat16
I32 = mybir.dt.int32
AF = mybir.ActivationFunctionType
ALU = mybir.AluOpType
AX = mybir.AxisListType


class T_:
    __slots__ = ("name", "lw", "rd")

    def __init__(self, name=""):
        self.name = name
        self.lw = None
        self.rd = []


class Op_:
    __slots__ = ("eng", "fn", "deps", "dma", "sig", "sigidx", "dsem", "dval", "idx", "guard")

    def __init__(self, eng, fn, dma):
        self.eng = eng
        self.fn = fn
        self.dma = dma
        self.deps = set()
        self.sig = False
        self.sigidx = 0
        self.dsem = None
        self.dval = 0
        self.guard = None


class Sched:
    ENGS = ("pe", "act", "dve", "pool", "sp")

    def __init__(self, nc, n_dma_sems=12, same_engine_sync=("act", "dve", "pool")):
        self.nc = nc
        self.ops = []
        self.n_dma_sems = n_dma_sems
        self.same = set(same_engine_sync)

    def T(self, name=""):
        return T_(name)

    def add(self, eng, fn, reads=(), writes=(), dma=False):
        op = Op_(eng, fn, dma)
        op.idx = len(self.ops)
        for t in reads:
            if t.lw is not None:
                op.deps.add(t.lw)
        for t in writes:
            if t.lw is not None:
                op.deps.add(t.lw)
            for r in t.rd:
                op.deps.add(r)
        for t in reads:
            t.rd.append(op.idx)
        for t in writes:
            t.lw = op.idx
            t.rd = []
        op.deps.discard(op.idx)
        self.ops.append(op)
        return op

    def dma(self, eng, out, in_, reads=(), writes=(), **kw):
        return self.add(eng, lambda e: e.dma_start(out=out, in_=in_, **kw), reads, writes, dma=True)

    def emit(self):
        nc = self.nc
        ops = self.ops
        for op in ops:
            for d in op.deps:
                dop = ops[d]
                if dop.dma:
                    dop.sig = True
                elif dop.eng != op.eng or op.eng in self.same or op.dma:
                    dop.sig = True
        cnt = {e: 0 for e in self.ENGS}
        ndq = {e: 0 for e in self.ENGS}
        ndma = 0
        NS = self.n_dma_sems
        for op in ops:
            if op.dma:
                op.sig = True
                q = "cc" if op.dma == "cc" else op.eng
                inc = 1 if op.dma == "cc" else 16
                m = ndq.setdefault(q, 0)
                op.dsem = (q, m % NS)
                op.dval = inc * (m // NS + 1)
                if m >= NS:
                    op.guard = (op.dsem, op.dval - inc)
                ndq[q] += 1
                ndma += 1
            elif op.sig:
                cnt[op.eng] += 1
                op.sigidx = cnt[op.eng]
        self.stats = dict(n_ops=len(ops), n_dma=ndma, sig=dict(cnt))
        with ExitStack() as es:
            esem = {e: es.enter_context(nc.semaphore("s_" + e)) for e in self.ENGS}
            dsems = {}
            for e in list(ndq.keys()):
                for i in range(min(NS, ndq[e])):
                    dsems[(e, i)] = es.enter_context(nc.semaphore("d_%s_%d" % (e, i)))
            block = es.enter_context(nc.Block())

            def run(engname, e):
                waited = {}

                def wait(key, sem, val):
                    if waited.get(key, 0) >= val:
                        return
                    waited[key] = val
                    e.wait_ge(sem, val)

                for op in ops:
                    if op.eng != engname:
                        continue
                    for d in sorted(op.deps):
                        dop = ops[d]
                        if dop.dma:
                            wait(("d", dop.dsem), dsems[dop.dsem], dop.dval)
                        elif dop.eng != engname or engname in self.same or op.dma:
                            wait(("e", dop.eng), esem[dop.eng], dop.sigidx)
                    if op.guard is not None:
                        wait(("d", op.guard[0]), dsems[op.guard[0]], op.guard[1])
                    if op.fn is None:
                        continue
                    ins = op.fn(e)
                    if op.dma:
                        ins.then_inc(dsems[op.dsem], 1 if op.dma == "cc" else 16)
                    elif op.sig:
                        ins.then_inc(esem[engname], 1)

            @block.tensor
            def _(e):
                run("pe", e)

            @block.scalar
            def _(e):
                run("act", e)

            @block.vector
            def _(e):
                run("dve", e)

            @block.gpsimd
            def _(e):
                run("pool", e)

            @block.sync
            def _(e):
                run("sp", e)


ALPHA = (2 * 4) ** 0.25
LN_EPS = 1e-5
NTOK = 2048
TT = 512
NT = NTOK // TT


class Arena:
    def __init__(self, nc, es, nbytes):
        self.t = es.enter_context(nc.sbuf_tensor("arena", [128, nbytes // 4], F32))
        self.off = 0
        self.nbytes = nbytes

    def seek(self, off):
        self.off = off

    def carve(self, shape, dt):
        esz = 4 if dt == F32 else 2
        n = 1
        for s in shape:
            n *= s
        nb = (n * esz + 31) // 32 * 32
        assert self.off % 4 == 0 and self.off + nb <= self.nbytes, (self.off, nb, self.nbytes)
        ap = self.t[:, self.off // 4:(self.off + nb) // 4]
        if dt != F32:
            ap = ap.bitcast(dt)
        ap = ap[:, 0:n]
        if len(shape) == 2:
            ap = ap.rearrange("p (a b) -> p a b", a=shape[0])
        elif len(shape) == 3:
            ap = ap.rearrange("p (a b c) -> p a b c", a=shape[0], b=shape[1])
        self.off += nb
        return ap


def V(S, eng, method, *args, reads=(), writes=(), **kw):
    return S.add(eng, lambda e: getattr(e, method)(*args, **kw), reads, writes)


def MM(S, out, pairs, reads=(), writes=()):
    def fn(e):
        n = len(pairs)
        ins = None
        for i, (l, r) in enumerate(pairs):
            ins = e.matmul(out, l, r, start=(i == 0), stop=(i == n - 1))
        return ins
    return S.add("pe", fn, reads, writes)


def barrier(S):
    last = {}
    dmas = {}
    for op in S.ops:
        if op.dma:
            dmas.setdefault((op.eng, op.dma == 'cc'), []).append(op.idx)
        elif op.fn is not None:
            last[op.eng] = op.idx
    for eng in S.ENGS:
        op = S.add(eng, None)
        for e2, idx in last.items():
            op.deps.add(idx)
        for q, lst in dmas.items():
            for d in lst[-S.n_dma_sems:]:
                op.deps.add(d)


def final_wait(S):
    op = S.add("sp", None)
    for o in S.ops[:-1]:
        if o.dma:
            op.deps.add(o.idx)


def layer_norm_tile(S, nc, X32, XB, hx, hxb, tt, sl, ones_bf, SQ, hsq, psA, hpsA, psB, hpsB, MEAN, hmean, TMP, htmp, RSTD, hrstd,
                    XC, hxc, G, Bv, eps_ap):
    for c in range(8):
        V(S, "act", "copy", XB[:, c, sl], X32[:, c, sl], reads=[hx[c][tt]], writes=[hxb[c][tt]])
        V(S, "act", "activation", SQ[:, c, :], X32[:, c, sl], AF.Square, reads=[hx[c][tt]], writes=[hsq[c]])
    MM(S, psA[:], [(ones_bf[:], XB[:, c, sl]) for c in range(8)], reads=[hxb[c][tt] for c in range(8)], writes=[hpsA])
    MM(S, psB[:], [(ones_bf[:], SQ[:, c, :]) for c in range(8)], reads=[hsq[c] for c in range(8)], writes=[hpsB])
    V(S, "act", "copy", MEAN[:], psA[:], reads=[hpsA], writes=[hmean])
    V(S, "dve", "tensor_tensor", TMP[:], MEAN[:], MEAN[:], ALU.mult, reads=[hmean], writes=[htmp])
    V(S, "dve", "scalar_tensor_tensor", TMP[:], psB[:], LN_EPS, TMP[:], ALU.add, ALU.subtract, reads=[hpsB, htmp], writes=[htmp])
    V(S, "act", "activation", TMP[:], TMP[:], AF.Sqrt, reads=[htmp], writes=[htmp])
    V(S, "dve", "reciprocal", RSTD[:], TMP[:], reads=[htmp], writes=[hrstd])
    for c in range(8):
        j = c % 2
        V(S, "dve", "tensor_tensor", XC[j][:], X32[:, c, sl], MEAN[:], ALU.subtract, reads=[hx[c][tt], hmean], writes=[hxc[j]])
        V(S, "pool", "tensor_tensor", XC[j][:], XC[j][:], RSTD[:], ALU.mult, reads=[hxc[j], hrstd], writes=[hxc[j]])
        V(S, "act", "activation", X32[:, c, sl], XC[j][:], AF.Identity, bias=Bv[:, c:c + 1], scale=G[:, c:c + 1],
          reads=[hxc[j]], writes=[hx[c][tt]])
        V(S, "pool", "tensor_copy", XB[:, c, sl], X32[:, c, sl], reads=[hx[c][tt]], writes=[hxb[c][tt]])


GP = int(os.environ.get('GP', '9'))
ST = os.environ.get('MST', 'rot,conv,tok,cols,ret,pool,gdn,gscan,gout').split(',')

SEQ = 4096
NTILE_M = SEQ // 512
NCOL = 2308
RMS_EPS = 1e-6
GN_EPS = 1e-5
C_ID, C_TRI, C_GCROW, C_NW, C_M2, C_M1, C_ONES, C_ONES2 = 0, 128, 256, 384, 512, 640, 768, 896
C_COLS = 1024
C32_N = 1024 + 8
B_ID, B_CAUS, B_MD, B_MO, B_MD0, B_ONES = 0, 128, 256, 512, 768, 1024
CBF_N = 1152


def emit_T(C, l):
    nc, S, A, ps, hps = C.nc, C.S, C.A, C.ps, C.hps
    X32, hx, LNP, hlnp, ONESB, ONES32, hones, base = C.X32, C.hx, C.LNP, C.hlnp, C.ONESB, C.ONES32, C.hones, C.base
    D = C.D
    wg = D["wg"][l]; wbr = D["wbr"][l]; wout = D["wout"][l]; lnp = D["lnp"][l]; wr = D["wr"][l]; br = D["br"][l]
    wge = D["wge"][l]; wue = D["wue"][l]; wde = D["wde"][l]; ident = D["ident"]; sel = D["sel"]
    x32v = D["x32h"].rearrange("(c p) t -> p c t", p=128)
    xo32v = D["xo32T"].rearrange("(c p) t -> p c t", p=128)
    if True:
        A.seek(base)
        S.dma("sp", LNP, lnp, writes=[hlnp])
        WG = A.carve([8, 3072], BF16); hwg = S.T("wg")
        WBR = A.carve([12, 1024], BF16); hwbr = S.T("wbr")
        WOUT = A.carve([8, 1024], BF16); hwout = S.T("wout")
        XIN = A.carve([8, TT], BF16); hxin = S.T("xin")
        YT = A.carve([12, TT], BF16); hyt = S.T("yt")
        YA = A.carve([4, TT], BF16); hya = S.T("ya")
        YB = A.carve([4, TT], BF16); hyb = S.T("yb")
        GT = [A.carve([3, TT], BF16) for _ in range(2)]; hgt = [S.T("gt%d" % i) for i in range(2)]
        M32 = [A.carve([TT], F32) for _ in range(2)]; hm32 = [S.T("m32%d" % i) for i in range(2)]
        T1 = [A.carve([TT], F32) for _ in range(2)]; ht1 = [S.T("t1%d" % i) for i in range(2)]
        MIX = A.carve([8, TT], BF16); hmix = [S.T("mix%d" % c) for c in range(8)]
        print("phase A arena end", A.off)
        wgv = wg.rearrange("(k p) n -> p k n", p=128)
        wbrv = wbr.rearrange("(k p) n -> p k n", p=128)
        hwg = [[S.T() for _r in range(3)] for _ in range(8)]; hwbr = [S.T() for _ in range(8)]
        for dc in range(8):
            for r in range(3):
                cc_ = r * 1024 + dc * 128
                S.dma("pool", WG[:, :, cc_:cc_ + 128], wgv[:, :, cc_:cc_ + 128], writes=[hwg[dc][r]])
            S.dma("pool", WBR[:, :, dc * 128:(dc + 1) * 128], wbrv[:, :, dc * 128:(dc + 1) * 128], writes=[hwbr[dc]])
        woutv = wout.rearrange("(k p) n -> p k n", p=128)
        for k in range(8):
            S.dma("pool", WOUT[:, k, :], woutv[:, k, :], writes=[hwout])
        pi = 0
        for tt in range(NT):
            sl = slice(tt * TT, (tt + 1) * TT)
            if l == 0:
                S.dma("pool", XIN, x32v[:, :, sl], writes=[hxin])
            else:
                S.dma("sp", XIN, C.xmine_v[(l - 1) % 2][tt // 2][:, :, (tt % 2) * TT:(tt % 2 + 1) * TT], reads=[C.hxmine[(l - 1) % 2][tt // 2]], writes=[hxin])
            YTv = YT.rearrange("p (r q) t -> p r q t", r=3)
            for r in range(3):
                for (dst, hdst, half) in ((YA, hya, 0), (YB, hyb, 1)):
                    for s_ in range(2):
                        jq = half * 2 + tt // 2
                        S.dma("sp", dst[:, 2 * s_:2 * s_ + 2, :], C.gy_v[l % 2][jq][s_][:, 2 * r:2 * r + 2, (tt % 2) * TT:(tt % 2 + 1) * TT],
                              reads=[C.hgy[l % 2][jq]], writes=[hdst])
                V(S, "dve", "tensor_scalar", YA, YA, C.SELM[:, 0:1], None, ALU.mult, reads=[hya, C.hselm], writes=[hya])
                V(S, "dve", "scalar_tensor_tensor", YTv[:, r, :, :], YB, C.SELM[:, 1:2], YA, ALU.mult, ALU.add,
                  reads=[hyb, hya, C.hselm], writes=[hyt])
            for dc in range(8):
                j = dc % 2
                for r in range(3):
                    b = pi % 4; pi += 1
                    col = r * 1024 + dc * 128
                    MM(S, ps[b][:], [(WG[:, k, col:col + 128], XIN[:, k, :]) for k in range(8)],
                       reads=[hwg[dc][r], hxin], writes=[hps[b]])
                    V(S, "act", "activation", GT[j][:, r, :], ps[b][:], AF.Sigmoid, reads=[hps[b]], writes=[hgt[j]])
                for r in range(3):
                    b = pi % 4; pi += 1
                    MM(S, ps[b][:], [(WBR[:, r * 4 + k, dc * 128:(dc + 1) * 128], YT[:, r * 4 + k, :]) for k in range(4)],
                       reads=[hwbr[dc], hyt], writes=[hps[b]])
                    if r == 0:
                        V(S, "dve", "tensor_tensor", M32[j], ps[b][:], GT[j][:, r, :], ALU.mult,
                          reads=[hps[b], hgt[j]], writes=[hm32[j]])
                    elif r == 1:
                        V(S, "dve", "tensor_tensor", T1[j], ps[b][:], GT[j][:, r, :], ALU.mult,
                          reads=[hps[b], hgt[j]], writes=[ht1[j]])
                        V(S, "pool", "tensor_tensor", M32[j], M32[j], T1[j], ALU.add,
                          reads=[hm32[j], ht1[j]], writes=[hm32[j]])
                    else:
                        V(S, "dve", "tensor_tensor", T1[j], ps[b][:], GT[j][:, r, :], ALU.mult,
                          reads=[hps[b], hgt[j]], writes=[ht1[j]])
                        V(S, "pool", "tensor_tensor", MIX[:, dc, :], M32[j], T1[j], ALU.add,
                          reads=[hm32[j], ht1[j]], writes=[hmix[dc]])
            for oc in range(8):
                b = 4 + oc % 2
                MM(S, ps[b][:], [(WOUT[:, k, oc * 128:(oc + 1) * 128], MIX[:, k, :]) for k in range(8)],
                   reads=[hwout] + hmix, writes=[hps[b]])
                V(S, "dve", "scalar_tensor_tensor", X32[:, oc, sl], X32[:, oc, sl], ALPHA, ps[b][:], ALU.mult, ALU.add,
                  reads=[hx[oc][tt], hps[b]], writes=[hx[oc][tt]])
        barrier(S)
        A.seek(base)
        XB = A.carve([8, NTOK], BF16)
        hxb = [[S.T("xb%d_%d" % (c, t)) for t in range(NT)] for c in range(8)]
        WE = []
        for i in range(2):
            WE.append((A.carve([8, 512], BF16), A.carve([8, 512], BF16), A.carve([4, 1024], BF16)))
        hwe = [(S.T(), S.T(), S.T()) for i in range(2)]
        WR = A.carve([8, 20], F32); hwr = S.T()
        BR = A.carve([20], F32); hbr = S.T()
        ID32 = A.carve([128], F32); hid = S.T()
        SEL = A.carve([2048], F32); hsel = S.T()
        LOG = A.carve([16, 20], F32); hlog = S.T()
        COMB = A.carve([16, 16], F32); hcomb = S.T()
        COMBT = A.carve([NTOK], F32); hcombt = S.T()
        scratch = A.off
        SQ = A.carve([8, TT], BF16); hsq = [S.T() for c in range(8)]
        MEAN = A.carve([TT], F32); hmean = S.T()
        TMP = A.carve([TT], F32); htmp = S.T()
        RSTD = A.carve([TT], F32); hrstd = S.T()
        XC = [A.carve([TT], F32) for _ in range(2)]; hxc = [S.T(), S.T()]
        end1 = A.off
        A.seek(scratch)
        CB = [A.carve([NTOK], BF16) for _ in range(2)]; hcb = [S.T(), S.T()]
        G1 = [A.carve([TT], BF16) for _ in range(2)]; hg1 = [S.T(), S.T()]
        G2 = [A.carve([TT], BF16) for _ in range(2)]; hg2 = [S.T(), S.T()]
        H = [A.carve([4, TT], BF16) for _ in range(2)]; hh = [[S.T() for _ in range(4)] for _ in range(2)]
        A.seek(max(A.off, end1))
        R = {n: A.carve([16, 4], F32) for n in ["ohg", "ec", "fsel", "oh1", "fs2", "oh2", "we", "wa"]}
        R16 = A.carve([16, 16], F32)
        Rs = {n: A.carve([16], F32) for n in ["cmax", "sumc", "pg", "m1", "m2", "dm", "w1", "w2", "cw1", "cw2"]}
        hr = S.T("route")
        print("phase B arena end", A.off)

        S.dma("sp", WR, wr.rearrange("(k p) n -> p k n", p=128), writes=[hwr])
        S.dma("sp", BR[0:1, :], br, writes=[hbr])
        S.dma("sp", ID32, ident, writes=[hid])
        S.dma("sp", SEL[0:16, :], sel, writes=[hsel])

        def load_expert(e):
            i = e % 2
            S.dma("pool", WE[i][0], wge[e].rearrange("(k p) n -> p k n", p=128), writes=[hwe[i][0]])
            S.dma("pool", WE[i][1], wue[e].rearrange("(k p) n -> p k n", p=128), writes=[hwe[i][1]])
            S.dma("pool", WE[i][2], wde[e].rearrange("(k p) n -> p k n", p=128), writes=[hwe[i][2]])
        load_expert(0)
        G1g, B1g, G2g, B2g = LNP[:, 0:8], LNP[:, 8:16], LNP[:, 16:24], LNP[:, 24:32]
        for tt in range(NT):
            sl = slice(tt * TT, (tt + 1) * TT)
            layer_norm_tile(S, nc, X32, XB, hx, hxb, tt, sl, ONESB, SQ, hsq, ps[4], hps[4], ps[5], hps[5], MEAN, hmean, TMP, htmp,
                            RSTD, hrstd, XC, hxc, G1g, B1g, None)
        barrier(S)
        load_expert(1)
        for blk in range(16):
            tt = blk // 4
            bs = slice(blk * 128, (blk + 1) * 128)
            b = blk % 2
            pairs = [(X32[:, c, bs], WR[:, c, :]) for c in range(8)] + [(ONES32[0:1, :], BR[0:1, :])]
            MM(S, ps[b][:, 0:20], pairs, reads=[hx[c][tt] for c in range(8)] + [hwr, hbr, hones], writes=[hps[b]])
            V(S, "act", "copy", LOG[:, blk, :], ps[b][:, 0:20], reads=[hps[b]], writes=[hlog])
        LC = LOG[:, :, 0:4]
        LF = LOG[:, :, 4:20].rearrange("p b (g e) -> p b g e", g=4)
        bc = lambda ap: ap.unsqueeze(2).to_broadcast([128, 16, 4])
        rw = dict(reads=[hlog, hr], writes=[hr])
        V(S, "dve", "tensor_reduce", Rs["cmax"], LC, AX.X, ALU.max, **rw)
        V(S, "dve", "tensor_tensor", R["ohg"], LC, bc(Rs["cmax"]), ALU.is_equal, **rw)
        V(S, "dve", "tensor_tensor", R["ec"], LC, bc(Rs["cmax"]), ALU.subtract, **rw)
        V(S, "act", "activation", R["ec"], R["ec"], AF.Exp, **rw)
        V(S, "dve", "tensor_reduce", Rs["sumc"], R["ec"], AX.X, ALU.add, **rw)
        V(S, "dve", "reciprocal", Rs["pg"], Rs["sumc"], **rw)
        R16v = R16.rearrange("p b (g e) -> p b g e", g=4)
        V(S, "dve", "tensor_tensor", R16v, LF, R["ohg"].unsqueeze(3).to_broadcast([128, 16, 4, 4]), ALU.mult, **rw)
        V(S, "dve", "tensor_reduce", R["fsel"], R16.rearrange("p b (g e) -> p b e g", g=4), AX.X, ALU.add, **rw)
        V(S, "dve", "tensor_reduce", Rs["m1"], R["fsel"], AX.X, ALU.max, **rw)
        V(S, "dve", "tensor_tensor", R["oh1"], R["fsel"], bc(Rs["m1"]), ALU.is_equal, **rw)
        V(S, "dve", "scalar_tensor_tensor", R["fs2"], R["oh1"], -1e30, R["fsel"], ALU.mult, ALU.add, **rw)
        V(S, "dve", "tensor_reduce", Rs["m2"], R["fs2"], AX.X, ALU.max, **rw)
        V(S, "dve", "tensor_tensor", R["oh2"], R["fs2"], bc(Rs["m2"]), ALU.is_equal, **rw)
        V(S, "dve", "tensor_tensor", Rs["dm"], Rs["m2"], Rs["m1"], ALU.subtract, **rw)
        V(S, "act", "activation", Rs["dm"], Rs["dm"], AF.Exp, **rw)
        V(S, "dve", "tensor_scalar", Rs["w1"], Rs["dm"], 1.0, None, ALU.add, **rw)
        V(S, "dve", "reciprocal", Rs["w1"], Rs["w1"], **rw)
        V(S, "dve", "tensor_tensor", Rs["w2"], Rs["dm"], Rs["w1"], ALU.mult, **rw)
        V(S, "dve", "tensor_tensor", Rs["cw1"], Rs["w1"], Rs["pg"], ALU.mult, **rw)
        V(S, "dve", "tensor_tensor", Rs["cw2"], Rs["w2"], Rs["pg"], ALU.mult, **rw)
        V(S, "dve", "tensor_tensor", R["wa"], R["oh1"], bc(Rs["cw1"]), ALU.mult, **rw)
        V(S, "dve", "tensor_tensor", R["we"], R["oh2"], bc(Rs["cw2"]), ALU.mult, **rw)
        V(S, "dve", "tensor_tensor", R["we"], R["we"], R["wa"], ALU.add, **rw)
        COMBv = COMB.rearrange("p b (g e) -> p b g e", g=4)
        V(S, "dve", "tensor_tensor", COMBv, R["ohg"].unsqueeze(3).to_broadcast([128, 16, 4, 4]),
          R["we"].unsqueeze(2).to_broadcast([128, 16, 4, 4]), ALU.mult, reads=[hr], writes=[hcomb])
        for blk in range(16):
            b = blk % 2
            S.add("pe", lambda e, b=b, blk=blk: e.transpose(ps[b][0:16, 0:128], COMB[:, blk, :], ID32),
                  reads=[hcomb, hid], writes=[hps[b]])
            V(S, "act", "copy", COMBT[0:16, blk * 128:(blk + 1) * 128], ps[b][0:16, 0:128], reads=[hps[b]], writes=[hcombt])
        pi = 0
        for e in range(16):
            i = e % 2
            Wg, Wu, Wd = WE[i]
            for tt in range(NT):
                sl = slice(tt * TT, (tt + 1) * TT)
                MM(S, ps[4][:], [(SEL[0:16, e * 128:(e + 1) * 128], COMBT[0:16, sl])], reads=[hsel, hcombt], writes=[hps[4]])
                V(S, "act", "copy", CB[i][:, sl], ps[4][:], reads=[hps[4]], writes=[hcb[i]])
            for tt in range(NT):
                sl = slice(tt * TT, (tt + 1) * TT)
                hb = tt % 2
                for hc in range(4):
                    j = pi % 2; pi += 1
                    bg, bu = j, 2 + j
                    MM(S, ps[bg][:], [(Wg[:, k, hc * 128:(hc + 1) * 128], XB[:, k, sl]) for k in range(8)],
                       reads=[hwe[i][0]] + [hxb[k][tt] for k in range(8)], writes=[hps[bg]])
                    MM(S, ps[bu][:], [(Wu[:, k, hc * 128:(hc + 1) * 128], XB[:, k, sl]) for k in range(8)],
                       reads=[hwe[i][1]] + [hxb[k][tt] for k in range(8)], writes=[hps[bu]])
                    V(S, "act", "activation", G1[j], ps[bg][:], AF.Silu, reads=[hps[bg]], writes=[hg1[j]])
                    V(S, "pool", "tensor_tensor", G2[j], G1[j], CB[i][:, sl], ALU.mult, reads=[hg1[j], hcb[i]], writes=[hg2[j]])
                    V(S, "dve", "tensor_tensor", H[hb][:, hc, :], ps[bu][:], G2[j], ALU.mult,
                      reads=[hps[bu], hg2[j]], writes=[hh[hb][hc]])
                for oc in range(8):
                    b = 4 + oc % 2
                    MM(S, ps[b][:], [(Wd[:, hc, oc * 128:(oc + 1) * 128], H[hb][:, hc, :]) for hc in range(4)],
                       reads=[hwe[i][2]] + hh[hb], writes=[hps[b]])
                    if e == 0:
                        V(S, "dve", "scalar_tensor_tensor", X32[:, oc, sl], X32[:, oc, sl], ALPHA, ps[b][:], ALU.mult, ALU.add,
                          reads=[hx[oc][tt], hps[b]], writes=[hx[oc][tt]])
                    else:
                        V(S, "dve", "tensor_tensor", X32[:, oc, sl], X32[:, oc, sl], ps[b][:], ALU.add,
                          reads=[hx[oc][tt], hps[b]], writes=[hx[oc][tt]])
            if e + 2 < 16:
                load_expert(e + 2)
        barrier(S)
        for tt in range(NT):
            sl = slice(tt * TT, (tt + 1) * TT)
            layer_norm_tile(S, nc, X32, XB, hx, hxb, tt, sl, ONESB, SQ, hsq, ps[4], hps[4], ps[5], hps[5], MEAN, hmean, TMP, htmp,
                            RSTD, hrstd, XC, hxc, G2g, B2g, None)
            if l == NL_FUSED - 1:
                for c in range(8):
                    S.dma("sp", xo32v[:, c, sl], X32[:, c, sl], reads=[hx[c][tt]])
            else:
                S.dma("sp", C.xmine_v[l % 2][tt // 2][:, :, (tt % 2) * TT:(tt % 2 + 1) * TT], XB[:, :, sl], reads=[hxb[c][tt] for c in range(8)],
                      writes=[C.hxmine[l % 2][tt // 2]])


class Buf:
    __slots__ = ("ap", "h")

    def __init__(self, ap, h):
        self.ap = ap
        self.h = h

    def __getitem__(self, k):
        return self.ap[k]


def emit_M(C, l):
    nc, S, A = C.nc, C.S, C.A
    D = C.D
    w = D["w"][l]; tabs = D["tabs"]; convw = D["convw"][l]; c32 = D["c32"][l]; cbf = D["cbf"]; poolw = D["poolw"][l]
    xv = D["x32f"].rearrange("(c p) t -> p c t", p=128)
    wv = w.rearrange("(k p) n -> p k n", p=128)
    if True:
        A.seek(C.base)

        def sb(shape, dt, name=""):
            return Buf(A.carve(shape, dt), S.T(name))
        pbig = [Buf(C.ps[i][:], C.hps[i]) for i in range(2)]
        psm = [Buf(C.ps[2 + i][:, 0:128], C.hps[2 + i]) for i in range(4)]
        ptb = [Buf(C.ptb[i][:, 0:128], C.hptb[i]) for i in range(2)]
        cnt = {"big": 0, "sm": 0, "tb": 0}

        def PB():
            cnt["big"] += 1
            return pbig[cnt["big"] % 2]

        def PS():
            cnt["sm"] += 1
            return psm[cnt["sm"] % 4]

        def PT_():
            cnt["tb"] += 1
            return ptb[cnt["tb"] % 2]

        def op(eng, method, *args, r=(), w=(), **kw):
            return S.add(eng, lambda e: getattr(e, method)(*args, **kw), [b.h for b in r], [b.h for b in w])

        def mm(out, pairs, r=(), w=()):
            def fn(e):
                n = len(pairs)
                ins = None
                for i, (l, rr) in enumerate(pairs):
                    ins = e.matmul(out, l, rr, start=(i == 0), stop=(i == n - 1))
                return ins
            return S.add("pe", fn, [b.h for b in r], [b.h for b in w])

        def tr(out, in_, ident, r=(), w=()):
            return S.add("pe", lambda e: e.transpose(out, in_, ident), [b.h for b in r], [b.h for b in w])

        def dma(eng, out, in_, r=(), w=()):
            return S.dma(eng, out, in_, reads=[b.h for b in r], writes=[b.h for b in w])

        W = sb([8, NCOL], BF16, "W")
        C32 = sb([C32_N], F32, "C32")
        CBF = sb([CBF_N], BF16, "CBF")
        CW = sb([24], F32, "CW")
        PW = sb([2, 128], BF16, "PW")
        NEGA = sb([2], F32, "NEGA")
        SRET = sb([128], F32, "SRET"); SRETB = sb([128], BF16, "SRETB")
        SG = [sb([128], F32, "SG%d" % h) for h in range(2)]
        SGB = [sb([128], BF16, "SGB%d" % h) for h in range(2)]
        UPREV = sb([256], BF16, "UPREV")
        CIN = [sb([3 + TT], F32, "CIN%d" % g) for g in range(6)]
        Wg_ = [Buf(W.ap, S.T()) for _ in range(3)]
        wbuf = lambda col: Wg_[0 if col < 512 else (1 if col < 1280 else 2)]
        for gi_, (c0_, c1_) in enumerate(((0, 512), (512, 1280), (1280, NCOL))):
            dma("pool", W[:, :, c0_:c1_], wv[:, :, c0_:c1_], w=[Wg_[gi_]])
        dma("sp", C32.ap, c32, w=[C32])
        dma("pool", CBF.ap, cbf, w=[CBF])
        dma("sp", CW.ap, convw, w=[CW])
        dma("pool", PW.ap, poolw.rearrange("g c d -> c g d"), w=[PW])
        ID32 = C32[:, C_ID:C_ID + 128]; TRI = C32[:, C_TRI:C_TRI + 128]; GCROW = C32[:, C_GCROW:C_GCROW + 128]
        NW = C32[:, C_NW:C_NW + 128]; M2 = C32[:, C_M2:C_M2 + 128]; M1 = C32[:, C_M1:C_M1 + 128]
        ONESF = C32[:, C_ONES:C_ONES + 128]
        ONES2 = C32[:, C_ONES2:C_ONES2 + 128]
        GCCOL = C32[:, C_COLS:C_COLS + 1]; DTB = C32[:, C_COLS + 1:C_COLS + 3]; ALOG = C32[:, C_COLS + 3:C_COLS + 5]
        PSC = C32[:, C_COLS + 5:C_COLS + 7]
        ZEROC = C32[:, C_COLS + 7:C_COLS + 8]
        IDB = CBF[:, B_ID:B_ID + 128]; CAUS = CBF[:, B_CAUS:B_CAUS + 128]; ONESB = CBF[:, B_ONES:B_ONES + 128]
        op("act", "activation", NEGA.ap, ALOG, AF.Exp, r=[C32], w=[NEGA])
        op("dve", "tensor_scalar", NEGA.ap, NEGA.ap, -1.0, None, ALU.mult, r=[NEGA], w=[NEGA])
        op("dve", "memset", SRET.ap, 0.0, w=[SRET]); op("dve", "memset", SRETB.ap, 0.0, w=[SRETB])
        for h in range(2):
            op("dve", "memset", SG[h].ap, 0.0, w=[SG[h]]); op("dve", "memset", SGB[h].ap, 0.0, w=[SGB[h]])
        op("dve", "memset", UPREV.ap, 0.0, w=[UPREV])
        for g in range(6):
            op("pool", "memset", CIN[g].ap, 0.0, w=[CIN[g]])
        XT = sb([8, TT], BF16, "XT")
        TAB = sb([4, TT], F32, "TAB")
        RT1 = sb([TT], F32); RT2 = sb([TT], F32)
        QTr = sb([TT], BF16); KTr = sb([TT], BF16)
        ACC = [sb([TT], F32) for _ in range(2)]
        CO = [sb([TT], F32) for _ in range(2)]
        SQB = sb([TT], BF16); RN = sb([TT], F32)
        QH = [sb([TT], BF16) for _ in range(2)]; KH = [sb([TT], BF16) for _ in range(2)]; VTB = [sb([TT], BF16) for _ in range(2)]
        RV = sb([4, 256], BF16); RGS = sb([4, 256], F32); PU = sb([4, 256], BF16); ZS = sb([4, 256], F32)
        BA = sb([4, 4], F32)
        COLS = {n: sb([4, 2], F32, n) for n in ["beta", "g", "gc", "egc", "kb", "kd", "t"]}
        SM = sb([128], BF16); KTOK = sb([128], BF16)
        ORET = sb([8, 128], F32); MVR = sb([8, 2], F32); RSTDR = sb([8], F32)
        YTOK = sb([128], BF16)
        SQ8 = sb([8, 128], F32)
        PTt = sb([2, TT], BF16)
        YOUT = sb([6, TT], BF16)
        gdL = [{n: sb([128], BF16, n) for n in ["KBG", "KDEC", "VB", "A", "AT", "R0", "R1", "P0", "P1", "Q0", "Q1", "WT", "QDEC", "ATT", "VN"]} for _ in range(2)]
        gfL = [{n: sb([128], F32, n) for n in ["GTRI", "EGC", "E1", "E2", "U"]} for _ in range(2)]
        OG = ORET; RSTDG = sb([8], F32)
        for g__ in gfL:
            op("pool", "memset", g__["E1"].ap, 0.0, w=[g__["E1"]]); op("pool", "memset", g__["E2"].ap, 0.0, w=[g__["E2"]])
        print("M arena end", A.off)
        if len(ST) < 9:
            op("pool", "memset", YOUT.ap, 0.0, w=[YOUT])

        for tt in range(NTILE_M):
            sl = slice(tt * TT, (tt + 1) * TT)
            if l == 0:
                dma("pool", XT.ap, xv[:, :, sl], w=[XT])
            else:
                jx = (tt % 4) // 2
                S.dma("sp", XT.ap, C.gx_v[(l - 1) % 2][jx][tt // 4][:, :, (tt % 2) * TT:(tt % 2 + 1) * TT], reads=[C.hgx[(l - 1) % 2][jx]], writes=[XT.h])
            dma("sp", TAB.ap, tabs[:, :, sl].rearrange("f p t -> p f t"), w=[TAB])

            def projF(col, M=128):
                pb = PB()
                mm(pb[0:M, :], [(W[:, k, col:col + M], XT[:, k, :]) for k in range(8)], r=[wbuf(col), XT], w=[pb])
                return pb
            if 'rot' in ST:
                for (c0, ci, dst) in ((0, 0, QTr), (256, 2, KTr)):
                    p1 = projF(c0); p2 = projF(c0 + 128)
                    op("dve", "tensor_tensor", RT1.ap, p1.ap, TAB[:, ci, :], ALU.mult, r=[p1, TAB], w=[RT1])
                    op("dve", "tensor_tensor", RT2.ap, p2.ap, TAB[:, ci + 1, :], ALU.mult, r=[p2, TAB], w=[RT2])
                    op("pool", "tensor_tensor", dst.ap, RT1.ap, RT2.ap, ALU.add, r=[RT1, RT2], w=[dst])
            if 'conv' in ST:
                for g in range(6):
                    kind, h = g // 2, g % 2
                    pb = projF(512 + g * 128)
                    ci = CIN[g]
                    op("act", "copy", ci[:, 3:3 + TT], pb.ap, r=[pb], w=[ci])
                    a = ACC[g % 2]
                    eng = "dve"
                    op(eng, "tensor_scalar", a.ap, ci[:, 0:TT], CW[:, g * 4:g * 4 + 1], None, ALU.mult, r=[ci, CW], w=[a])
                    for j in range(1, 4):
                        op(eng, "scalar_tensor_tensor", a.ap, ci[:, j:j + TT], CW[:, g * 4 + j:g * 4 + j + 1], a.ap, ALU.mult, ALU.add,
                           r=[ci, CW, a], w=[a])
                    op("pool", "tensor_copy", ci[:, 0:3], ci[:, TT:TT + 3], r=[ci, a], w=[ci])
                    co = CO[g % 2]
                    op("act", "activation", co.ap, a.ap, AF.Silu, r=[a], w=[co])
                    if kind < 2:
                        op("act", "activation", SQB.ap, co.ap, AF.Square, r=[co], w=[SQB])
                        pb2 = PB()
                        mm(pb2.ap, [(ONESB, SQB.ap)], r=[CBF, SQB], w=[pb2])
                        op("dve", "tensor_scalar", RN.ap, pb2.ap, RMS_EPS, None, ALU.add, r=[pb2], w=[RN])
                        op("act", "activation", RN.ap, RN.ap, AF.Sqrt, r=[RN], w=[RN])
                        op("dve", "reciprocal", RN.ap, RN.ap, r=[RN], w=[RN])
                        dst = QH[h] if kind == 0 else KH[h]
                        sc = (128.0 ** -0.5) if kind == 0 else 1.0
                        op("dve", "scalar_tensor_tensor", dst.ap, co.ap, sc, RN.ap, ALU.mult, ALU.mult, r=[co, RN], w=[dst])
                    else:
                        op("pool", "tensor_copy", VTB[h].ap, co.ap, r=[co], w=[VTB[h]])
            if 'tok' in ST:
                for blk in range(4):
                    bs = slice(blk * 128, (blk + 1) * 128)
                    pa = PB()
                    mm(pa.ap, [(XT[:, k, bs], W[:, k, 1280:1792]) for k in range(8)], r=[wbuf(1280), XT], w=[pa])
                    op("act", "copy", RV[:, blk, :], pa[:, 0:256], r=[pa], w=[RV])
                    op("act", "activation", RGS[:, blk, :], pa[:, 256:512], AF.Silu, r=[pa], w=[RGS])
                    pb = PB()
                    mm(pb.ap, [(XT[:, k, bs], W[:, k, 1792:2304]) for k in range(8)], r=[wbuf(1280), XT], w=[pb])
                    op("act", "copy", PU[:, blk, :], pb[:, 0:256], r=[pb], w=[PU])
                    op("act", "activation", ZS[:, blk, :], pb[:, 256:512], AF.Silu, r=[pb], w=[ZS])
                    pc = PS()
                    mm(pc[:, 0:4], [(XT[:, k, bs], W[:, k, 2304:2308]) for k in range(8)], r=[wbuf(1280), XT], w=[pc])
                    op("dve", "tensor_copy", BA[:, blk, :], pc[:, 0:4], r=[pc], w=[BA])
            if 'cols' in ST:
                cb = COLS
                rw = lambda *bs_: dict(r=list(bs_), w=[bs_[-1]])
                op("act", "activation", cb["beta"].ap, BA[:, :, 0:2], AF.Exp, scale=-1.0, r=[BA], w=[cb["beta"]])
                op("dve", "tensor_scalar", cb["beta"].ap, cb["beta"].ap, 1.0, None, ALU.add, r=[cb["beta"]], w=[cb["beta"]])
                op("dve", "reciprocal", cb["beta"].ap, cb["beta"].ap, r=[cb["beta"]], w=[cb["beta"]])
                op("dve", "tensor_tensor", cb["g"].ap, BA[:, :, 2:4], DTB.unsqueeze(1).to_broadcast([128, 4, 2]), ALU.add, r=[BA, C32], w=[cb["g"]])
                op("act", "activation", cb["g"].ap, cb["g"].ap, AF.Exp, r=[cb["g"]], w=[cb["g"]])
                op("dve", "tensor_scalar", cb["g"].ap, cb["g"].ap, 1.0, None, ALU.add, r=[cb["g"]], w=[cb["g"]])
                op("act", "activation", cb["g"].ap, cb["g"].ap, AF.Ln, r=[cb["g"]], w=[cb["g"]])
                op("dve", "tensor_tensor", cb["g"].ap, cb["g"].ap, NEGA.ap.unsqueeze(1).to_broadcast([128, 4, 2]), ALU.mult, r=[cb["g"], NEGA], w=[cb["g"]])
                pg = PS()
                mm(pg[:, 0:8], [(TRI, cb["g"].ap.rearrange("p b h -> p (b h)"))], r=[C32, cb["g"]], w=[pg])
                op("dve", "tensor_copy", cb["gc"].ap.rearrange("p b h -> p (b h)"), pg[:, 0:8], r=[pg], w=[cb["gc"]])
                op("act", "activation", cb["egc"].ap, cb["gc"].ap, AF.Exp, r=[cb["gc"]], w=[cb["egc"]])
                op("dve", "tensor_tensor", cb["kb"].ap, cb["egc"].ap, cb["beta"].ap, ALU.mult, r=[cb["egc"], cb["beta"]], w=[cb["kb"]])
                op("dve", "tensor_scalar", cb["t"].ap, cb["gc"].ap, -1.0, None, ALU.mult, r=[cb["gc"]], w=[cb["t"]])
                pgl = PS()
                mm(pgl[:, 0:8], [(ONES2, cb["g"].ap.rearrange("p b h -> p (b h)"))], r=[C32, cb["g"]], w=[pgl])
                op("dve", "tensor_tensor", cb["kd"].ap.rearrange("p b h -> p (b h)"), pgl[:, 0:8], cb["gc"].ap.rearrange("p b h -> p (b h)"), ALU.subtract,
                   r=[pgl, cb["gc"]], w=[cb["kd"]])
                op("act", "activation", cb["kd"].ap, cb["kd"].ap, AF.Exp, r=[cb["kd"]], w=[cb["kd"]])

            if 'ret' in ST:
                for blk in range(4):
                    bs = slice(blk * 128, (blk + 1) * 128)
                    pt = PT_()
                    tr(pt.ap, KTr[:, bs], IDB, r=[KTr, CBF], w=[pt])
                    op("dve", "tensor_tensor", KTOK.ap, pt.ap, GCROW, ALU.mult, r=[pt, C32], w=[KTOK])
                    pkvb = PB(); pkv = Buf(pkvb.ap[:, 0:128], pkvb.h)
                    for h in range(2):
                        hs = slice(h * 64, (h + 1) * 64)
                        psc = PS()
                        mm(psc.ap, [(KTr[hs, bs], QTr[hs, bs])], r=[KTr, QTr], w=[psc])
                        op("dve", "tensor_tensor", SM.ap, psc.ap, CAUS, ALU.mult, r=[psc, CBF], w=[SM])
                        po = PS()
                        mm(po.ap, [(SM.ap, RV[:, blk, h * 128:(h + 1) * 128]), (QTr[hs, bs], SRETB[hs, :])], r=[SM, RV, QTr, SRETB], w=[po])
                        idx = blk * 2 + h
                        op("act", "copy", ORET[:, idx, :], po.ap, r=[po], w=[ORET])
                        mm(pkv[hs, :], [(KTOK[:, hs], RV[:, blk, h * 128:(h + 1) * 128])], r=[KTOK, RV], w=[pkv])
                    op("dve", "scalar_tensor_tensor", SRET.ap, SRET.ap, GCCOL, pkv.ap, ALU.mult, ALU.add, r=[SRET, C32, pkv], w=[SRET])
                    op("act", "copy", SRETB.ap, SRET.ap, r=[SRET], w=[SRETB])
                op("dve", "tensor_reduce", MVR[:, :, 0], ORET.ap, AX.X, ALU.add, r=[ORET], w=[MVR])
                op("act", "activation", SQ8.ap, ORET.ap, AF.Square, r=[ORET], w=[SQ8])
                op("dve", "tensor_reduce", MVR[:, :, 1], SQ8.ap, AX.X, ALU.add, r=[SQ8, MVR], w=[MVR])
                op("dve", "tensor_scalar", MVR.ap, MVR.ap, 1.0 / 128.0, None, ALU.mult, r=[MVR], w=[MVR])
                op("dve", "tensor_tensor", RSTDR.ap, MVR[:, :, 0], MVR[:, :, 0], ALU.mult, r=[MVR], w=[RSTDR])
                op("dve", "tensor_tensor", RSTDR.ap, MVR[:, :, 1], RSTDR.ap, ALU.subtract, r=[MVR, RSTDR], w=[RSTDR])
                op("dve", "tensor_scalar", RSTDR.ap, RSTDR.ap, GN_EPS, None, ALU.add, r=[RSTDR], w=[RSTDR])
                op("act", "activation", RSTDR.ap, RSTDR.ap, AF.Sqrt, r=[RSTDR], w=[RSTDR])
                op("dve", "reciprocal", RSTDR.ap, RSTDR.ap, r=[RSTDR], w=[RSTDR])
                for blk in range(4):
                    for h in range(2):
                        idx = blk * 2 + h
                        op("dve", "tensor_scalar", ORET[:, idx, :], ORET[:, idx, :], MVR[:, idx, 0:1], RSTDR[:, idx:idx + 1], ALU.subtract, ALU.mult,
                           r=[ORET, MVR, RSTDR], w=[ORET])
                        op("pool", "tensor_tensor", YTOK.ap, ORET[:, idx, :], RGS[:, blk, h * 128:(h + 1) * 128], ALU.mult, r=[ORET, RGS], w=[YTOK])
                        pt = PT_()
                        tr(pt.ap, YTOK.ap, IDB, r=[YTOK, CBF], w=[pt])
                        op("act", "copy", YOUT[:, h, blk * 128:(blk + 1) * 128], pt.ap, r=[pt], w=[YOUT])
            if 'pool' in ST:
                for g in range(2):
                    for blk in range(4):
                        pp = PS()
                        ug = PU[:, blk, g * 128:(g + 1) * 128]
                        if tt == 0 and blk == 0:
                            mm(pp.ap, [(ug, CBF[:, B_MD0 + g * 128:B_MD0 + (g + 1) * 128])], r=[PU, CBF], w=[pp])
                        else:
                            uprev = UPREV[:, g * 128:(g + 1) * 128] if blk == 0 else PU[:, blk - 1, g * 128:(g + 1) * 128]
                            mm(pp.ap, [(ug, CBF[:, B_MD + g * 128:B_MD + (g + 1) * 128]), (uprev, CBF[:, B_MO + g * 128:B_MO + (g + 1) * 128])],
                               r=[PU, UPREV, CBF], w=[pp])
                        op("act", "copy", PTt[:, g, blk * 128:(blk + 1) * 128], pp.ap, r=[pp], w=[PTt])
                    MP = int(os.environ.get("MPOOL", "4"))
                    if MP >= 2:
                        pb = PB()
                        mm(pb.ap, [(PW[:, g, :], PTt[:, g, :])], r=[PW, PTt], w=[pb])
                    if MP >= 3:
                        op("act", "activation", YOUT[:, 2 + g, :], pb.ap, AF.Identity, scale=PSC[:, g:g + 1], r=[pb, C32], w=[YOUT])
                if MP >= 4:
                    op("pool", "tensor_copy", UPREV.ap, PU[:, 3, :], r=[PU], w=[UPREV])
            if 'gdn' in ST:
                for blk in range(4):
                    bs = slice(blk * 128, (blk + 1) * 128)
                    def gdn_block(blk, h, bs):
                        gd = gdL[h]; gf = gfL[h]
                        gcol = cb["g"][:, blk, h:h + 1]; gccol = cb["gc"][:, blk, h:h + 1]
                        betac = cb["beta"][:, blk, h:h + 1]; kbc = cb["kb"][:, blk, h:h + 1]
                        yield
                        op("dve", "tensor_scalar", gf["GTRI"].ap, TRI, gcol, None, ALU.mult, r=[C32, cb["g"]], w=[gf["GTRI"]])
                        pgc = PS()
                        yield
                        mm(pgc.ap, [(ONESF, gf["GTRI"].ap)], r=[C32, gf["GTRI"]], w=[pgc])
                        yield
                        op("act", "activation", gf["EGC"].ap, pgc.ap, AF.Exp, r=[pgc], w=[gf["EGC"]])
                        for c_ in range(2):
                            rs_ = slice(c_ * 64, (c_ + 1) * 64)
                            yield
                            op("act", "activation", gf["E1"][rs_, rs_], pgc[rs_, rs_], AF.Exp, bias=cb["t"][rs_, blk, h:h + 1], r=[pgc, cb["t"], gf["E1"]], w=[gf["E1"]])
                            yield
                            op("act", "activation", gf["E2"][rs_, rs_], pgc[rs_, rs_], AF.Exp, bias=cb["gc"][rs_, blk, h:h + 1], scale=-1.0, r=[pgc, cb["gc"], gf["E2"]], w=[gf["E2"]])
                        yield
                        op("dve", "scalar_tensor_tensor", gf["E1"].ap, gf["E1"].ap, 1.0, M1, ALU.min, ALU.mult, r=[gf["E1"], C32], w=[gf["E1"]])
                        yield
                        op("dve", "scalar_tensor_tensor", gf["E2"].ap, gf["E2"].ap, 1.0, M2, ALU.min, ALU.mult, r=[gf["E2"], C32], w=[gf["E2"]])
                        if GP <= 1: return
                        ptk = PT_()
                        yield
                        tr(ptk.ap, KH[h][:, bs], IDB, r=[KH[h], CBF], w=[ptk])
                        yield
                        op("act", "activation", gd["KBG"].ap, ptk.ap, AF.Identity, scale=kbc, r=[ptk, cb["kb"]], w=[gd["KBG"]])
                        yield
                        op("act", "activation", gd["KDEC"].ap, ptk.ap, AF.Identity, scale=cb["kd"][:, blk, h:h + 1], r=[ptk, cb["kd"]], w=[gd["KDEC"]])
                        ptv = PT_()
                        yield
                        tr(ptv.ap, VTB[h][:, bs], IDB, r=[VTB[h], CBF], w=[ptv])
                        yield
                        op("act", "activation", gd["VB"].ap, ptv.ap, AF.Identity, scale=betac, r=[ptv, cb["beta"]], w=[gd["VB"]])
                        if GP <= 2: return
                        pkk = PS()
                        yield
                        mm(pkk.ap, [(KH[h][:, bs], KH[h][:, bs])], r=[KH[h]], w=[pkk])
                        yield
                        op("dve", "scalar_tensor_tensor", gd["A"].ap, pkk.ap, betac, gf["E2"].ap, ALU.mult, ALU.mult, r=[pkk, cb["beta"], gf["E2"]], w=[gd["A"]])
                        pqk = PS()
                        yield
                        mm(pqk.ap, [(KH[h][:, bs], QH[h][:, bs])], r=[KH[h], QH[h]], w=[pqk])
                        yield
                        op("dve", "tensor_tensor", gd["ATT"].ap, pqk.ap, gf["E1"].ap, ALU.mult, r=[pqk, gf["E1"]], w=[gd["ATT"]])
                        yield
                        op("dve", "tensor_tensor", gd["QDEC"].ap, QH[h][:, bs], gf["EGC"].ap, ALU.mult, r=[QH[h], gf["EGC"]], w=[gd["QDEC"]])
                        if GP <= 3: return
                        pat = PT_()
                        yield
                        tr(pat.ap, gd["A"].ap, IDB, r=[gd["A"], CBF], w=[pat])
                        yield
                        op("dve", "tensor_copy", gd["AT"].ap, pat.ap, r=[pat], w=[gd["AT"]])
                        yield
                        op("dve", "scalar_tensor_tensor", gd["R0"].ap, pat.ap, -1.0, IDB, ALU.mult, ALU.add, r=[CBF, pat], w=[gd["R0"]])
                        if GP <= 4: return
                        Pc, Qc, Rc = gd["A"], gd["AT"], gd["R0"]
                        Pn = [gd["P0"], gd["P1"]]; Qn = [gd["Q0"], gd["Q1"]]; Rn = [gd["R1"], gd["R0"]]
                        for k in range(1, 6):
                            pp_ = PS()
                            yield
                            mm(pp_.ap, [(Qc.ap, Pc.ap)], r=[Qc, Pc], w=[pp_])
                            Pnew = Pn[k % 2]
                            yield
                            op("act", "copy", Pnew.ap, pp_.ap, r=[pp_], w=[Pnew])
                            if k < 5:
                                pq_ = PS()
                                mm(pq_.ap, [(Pc.ap, Qc.ap)], r=[Qc, Pc], w=[pq_])
                                Qnew = Qn[k % 2]
                                op("dve", "tensor_copy", Qnew.ap, pq_.ap, r=[pq_], w=[Qnew])
                            pr_ = PS()
                            yield
                            mm(pr_.ap, [(IDB, Rc.ap), (Pnew.ap, Rc.ap)], r=[CBF, Rc, Pnew], w=[pr_])
                            Rnew = Rn[(k - 1) % 2]
                            yield
                            op("dve", "tensor_copy", Rnew.ap, pr_.ap, r=[pr_], w=[Rnew])
                            Pc, Rc = Pnew, Rnew
                            if k < 5:
                                Qc = Qnew
                        if GP <= 5: return
                        TTm = Rc
                        pw = PS()
                        yield
                        mm(pw.ap, [(gd["KBG"].ap, TTm.ap)], r=[gd["KBG"], TTm], w=[pw])
                        yield
                        op("act", "copy", gd["WT"].ap, pw.ap, r=[pw], w=[gd["WT"]])
                        pu_ = PS()
                        yield
                        mm(pu_.ap, [(TTm.ap, gd["VB"].ap)], r=[TTm, gd["VB"]], w=[pu_])
                        yield
                        op("act", "copy", gf["U"].ap, pu_.ap, r=[pu_], w=[gf["U"]])
                        if GP <= 6: return
                        for c in (range(2) if 'gscan' in ST else []):
                            rs = slice(c * 64, (c + 1) * 64)
                            pws = PS()
                            yield
                            mm(pws[rs, :], [(gd["WT"][:, rs], SGB[h].ap)], r=[gd["WT"], SGB[h]], w=[pws])
                            yield
                            op("dve", "scalar_tensor_tensor", gd["VN"][rs, :], pws[rs, :], -1.0, gf["U"][rs, :], ALU.mult, ALU.add, r=[gf["U"], pws, gd["VN"]], w=[gd["VN"]])
                            pog = PS()
                            yield
                            mm(pog[rs, :], [(gd["QDEC"][:, rs], SGB[h].ap), (gd["ATT"][rs, rs], gd["VN"][rs, :])],
                               r=[gd["QDEC"], SGB[h], gd["ATT"], gd["VN"]], w=[pog])
                            idx = blk * 2 + h
                            yield
                            op("act", "copy", OG[rs, idx, :], pog[rs, :], r=[pog, OG], w=[OG])
                            pds = PS()
                            yield
                            mm(pds.ap, [(gd["KDEC"][rs, :], gd["VN"][rs, :])], r=[gd["KDEC"], gd["VN"]], w=[pds])
                            yield
                            op("dve", "scalar_tensor_tensor", SG[h].ap, SG[h].ap, gf["EGC"][:, c * 64 + 63:c * 64 + 64], pds.ap, ALU.mult, ALU.add,
                               r=[SG[h], gf["EGC"], pds], w=[SG[h]])
                            yield
                            op("act", "copy", SGB[h].ap, SG[h].ap, r=[SG[h]], w=[SGB[h]])
                    gens = [gdn_block(blk, h, bs) for h in range(2)]
                    while gens:
                        for g_ in list(gens):
                            try:
                                next(g_)
                            except StopIteration:
                                gens.remove(g_)
            if 'gout' in ST:
                op("act", "activation", SQ8.ap, OG.ap, AF.Square, r=[OG], w=[SQ8])
                op("dve", "tensor_reduce", RSTDG.ap, SQ8.ap, AX.X, ALU.add, r=[SQ8], w=[RSTDG])
                op("dve", "tensor_scalar", RSTDG.ap, RSTDG.ap, 1.0 / 128.0, None, ALU.mult, r=[RSTDG], w=[RSTDG])
                op("dve", "tensor_scalar", RSTDG.ap, RSTDG.ap, RMS_EPS, None, ALU.add, r=[RSTDG], w=[RSTDG])
                op("act", "activation", RSTDG.ap, RSTDG.ap, AF.Sqrt, r=[RSTDG], w=[RSTDG])
                op("dve", "reciprocal", RSTDG.ap, RSTDG.ap, r=[RSTDG], w=[RSTDG])
                for blk in range(4):
                    for h in range(2):
                        idx = blk * 2 + h
                        zs = ZS[:, blk, h * 128:(h + 1) * 128]
                        op("pool", "tensor_tensor", zs, zs, NW, ALU.mult, r=[ZS, C32], w=[ZS])
                        op("dve", "scalar_tensor_tensor", YTOK.ap, OG[:, idx, :], RSTDG[:, idx:idx + 1], zs, ALU.mult, ALU.mult,
                           r=[OG, RSTDG, ZS], w=[YTOK])
                        pt = PT_()
                        tr(pt.ap, YTOK.ap, IDB, r=[YTOK, CBF], w=[pt])
                        op("act", "copy", YOUT[:, 4 + h, blk * 128:(blk + 1) * 128], pt.ap, r=[pt], w=[YOUT])
            S.dma("sp", C.ymine_v[l % 2][tt // 2][:, :, (tt % 2) * TT:(tt % 2 + 1) * TT], YOUT.ap, reads=[YOUT.h], writes=[C.hymine[l % 2][tt // 2]])


bf16 = ml_dtypes.bfloat16

def m_consts(hh):
    t = np.arange(4096)
    half = 32
    inv_freq = (np.float32(10000.0) ** (-(np.arange(half, dtype=np.float32)) / np.float32(half))).astype(np.float32)
    ang = (t.astype(np.float32)[:, None] * inv_freq[None, :]).astype(np.float32)
    cos = np.cos(ang).astype(np.float32).T; sin = np.sin(ang).astype(np.float32).T
    lg = np.log(1.0 - np.exp2(-5.0 - np.arange(4, dtype=np.float64)))
    j = (t % 128).astype(np.float64)
    tabs = np.zeros((4, 128, 4096), np.float32)
    gcrow = np.zeros((128, 128), np.float32); gccol = np.zeros((128, 1), np.float32)
    for h in range(2):
        H = 2 * hh + h
        xi = np.exp((j + 1) * lg[H]); kf = np.exp(-(j + 1) * lg[H]) * 64 ** -0.5
        r = slice(h * 64, h * 64 + 32); r2 = slice(h * 64 + 32, h * 64 + 64)
        tabs[0, r] = cos * xi; tabs[0, r2] = cos * xi
        tabs[1, r] = -sin * xi; tabs[1, r2] = sin * xi
        tabs[2, r] = cos * kf; tabs[2, r2] = cos * kf
        tabs[3, r] = -sin * kf; tabs[3, r2] = sin * kf
        gcrow[:, h * 64:(h + 1) * 64] = np.exp(128 * lg[H]); gccol[h * 64:(h + 1) * 64, 0] = np.exp(128 * lg[H])
    p = np.arange(128)[:, None]; f = np.arange(128)[None, :]
    same = (p // 64) == (f // 64)
    tri = (same & (p <= f)).astype(np.float32)
    m1 = (same & (f >= p)).astype(np.float32)
    m2 = (same & (p > f)).astype(np.float32)
    caus = (f >= p).astype(np.float32)
    md = np.zeros((2, 128, 128), np.float32); mo = np.zeros((2, 128, 128), np.float32); md0 = np.zeros((2, 128, 128), np.float32)
    for g in range(2):
        wdw = (2, 4, 8, 16)[2 * hh + g]
        for tq in range(128):
            for s_ in range(tq - wdw + 1, tq + 1):
                if s_ >= 0: md[g, s_, tq] += 1.0 / wdw
                else: mo[g, 128 + s_, tq] += 1.0 / wdw
            cntq = min(tq + 1, wdw)
            for s_ in range(max(0, tq - wdw + 1), tq + 1):
                md0[g, s_, tq] += 1.0 / cntq
            md[g, tq, tq] -= 1.0; md0[g, tq, tq] -= 1.0
    return dict(tabs=tabs, gcrow=gcrow, gccol=gccol, tri=tri, m1=m1, m2=m2, caus=caus, md=md, mo=mo, md0=md0, same=same.astype(np.float32))

def m_inputs(inp, l, b_x32T, hh, K):
    wi = inp["w_in"][l]
    hs = [2 * hh, 2 * hh + 1]
    cols = []
    rq = lambda base, h: np.arange(base + h * 64, base + (h + 1) * 64)
    sw = lambda a: np.concatenate([a[32:], a[:32]])
    cols += [np.concatenate([rq(0, h) for h in hs]), np.concatenate([sw(rq(0, h)) for h in hs])]
    cols += [np.concatenate([rq(256, h) for h in hs]), np.concatenate([sw(rq(256, h)) for h in hs])]
    for base in (2048, 2560, 3072):
        for h in hs: cols.append(np.arange(base + h * 128, base + (h + 1) * 128))
    for base in (512, 1024, 1536, 3584):
        cols.append(np.arange(base + hs[0] * 128, base + (hs[1] + 1) * 128))
    cols.append(np.array([4096 + hs[0], 4096 + hs[1], 4100 + hs[0], 4100 + hs[1]]))
    cols = np.concatenate(cols); assert len(cols) == 2308
    m = {"x32T": np.ascontiguousarray(b_x32T), "w": np.ascontiguousarray(wi[:, cols]), "tabs": K["tabs"]}
    cw = np.zeros((128, 24), np.float32)
    for g in range(6):
        kind, h = g // 2, g % 2
        base = kind * 512 + hs[h] * 128
        cw[:, g * 4:(g + 1) * 4] = inp["conv_w"][l][:, base:base + 128].T
    m["convw"] = cw
    c32 = np.zeros((128, C32_N), np.float32)
    c32[:, C_ID:C_ID + 128] = np.eye(128); c32[:, C_TRI:C_TRI + 128] = K["tri"]; c32[:, C_GCROW:C_GCROW + 128] = K["gcrow"]
    c32[:, C_NW:C_NW + 128] = inp["gdn_norm_w"][l][None, :]; c32[:, C_M2:C_M2 + 128] = K["m2"]; c32[:, C_M1:C_M1 + 128] = K["m1"]
    c32[:, C_ONES:C_ONES + 128] = 1.0; c32[:, C_ONES2:C_ONES2 + 128] = K["same"]
    c32[:, C_COLS] = K["gccol"][:, 0]
    for h in range(2):
        c32[:, C_COLS + 1 + h] = inp["dt_bias"][l][hs[h]]; c32[:, C_COLS + 3 + h] = inp["A_log"][l][hs[h]]
        c32[:, C_COLS + 5 + h] = inp["pool_scale"][l][hs[h] * 128:(hs[h] + 1) * 128]
    m["c32"] = c32
    cb = np.zeros((128, CBF_N), np.float32)
    cb[:, B_ID:B_ID + 128] = np.eye(128); cb[:, B_CAUS:B_CAUS + 128] = K["caus"]
    for g in range(2):
        cb[:, B_MD + g * 128:B_MD + (g + 1) * 128] = K["md"][g]; cb[:, B_MO + g * 128:B_MO + (g + 1) * 128] = K["mo"][g]
        cb[:, B_MD0 + g * 128:B_MD0 + (g + 1) * 128] = K["md0"][g]
    cb[:, B_ONES:B_ONES + 128] = 1.0
    m["cbf"] = cb
    m["poolw"] = np.ascontiguousarray(inp["pool_w"][l][hs[0]:hs[1] + 1])
    return m

def t_inputs(inp, l, x32T_half, yT_half):
    IN = 7176 - 3072
    m = {}
    m["x32T"] = np.ascontiguousarray(x32T_half, dtype=np.float32)
    m["yT"] = np.ascontiguousarray(yT_half).astype(bf16) if yT_half.dtype != bf16 else np.ascontiguousarray(yT_half)
    m["wg"] = np.ascontiguousarray(inp["w_in"][l][:, IN:])
    m["wbr"] = np.ascontiguousarray(inp["w_branch"][l].reshape(1536, 1024))
    m["wout"] = np.ascontiguousarray(inp["w_out"][l])
    f = lambda v: v.reshape(8, 128).T
    m["lnp"] = np.ascontiguousarray(np.concatenate([f(inp["ln1_g"][l]), f(inp["ln1_b"][l]), f(inp["ln2_g"][l]), f(inp["ln2_b"][l])], axis=1))
    m["wr"] = np.ascontiguousarray(np.concatenate([inp["router_coarse_w"][l], inp["router_fine_w"][l]], axis=1))
    m["br"] = np.ascontiguousarray(np.concatenate([inp["router_coarse_b"][l], inp["router_fine_b"][l]])[None, :])
    m["wge"] = inp["w_gate"][l]; m["wue"] = inp["w_up"][l]; m["wde"] = inp["w_down"][l]
    m["ident"] = np.eye(128, dtype=np.float32)
    sel = np.zeros((16, 16, 128), np.float32)
    for e in range(16): sel[e, e, :] = 1.0
    m["sel"] = sel.reshape(16, 2048)
    return m


class Ctx:
    pass


NL_FUSED = int(os.environ.get('FUSED_NL', '4'))
NC_FUSED = int(os.environ.get('FUSED_NC', '8'))


def build_fused():
    nc = bass.Bass("TRN2", target_bir_lowering=False)
    dr = lambda name, shape, dt, kind="ExternalInput": nc.dram_tensor(name, shape, dt, kind=kind).ap()
    D = {}
    D["x32f"] = dr("x32f", [1024, SEQ], F32)
    D["x32h"] = dr("x32h", [1024, NTOK], F32)
    D["selm"] = dr("selm", [128, 2], F32)
    D["w"] = dr("w", [4, 1024, NCOL], F32)
    D["tabs"] = dr("tabs", [4, 128, SEQ], F32)
    D["convw"] = dr("convw", [4, 128, 24], F32)
    D["c32"] = dr("c32", [4, 128, C32_N], F32)
    D["cbf"] = dr("cbf", [128, CBF_N], F32)
    D["poolw"] = dr("poolw", [4, 2, 128, 128], F32)
    D["wg"] = dr("wg", [4, 1024, 3072], F32)
    D["wbr"] = dr("wbr", [4, 1536, 1024], F32)
    D["wout"] = dr("wout", [4, 1024, 1024], F32)
    D["lnp"] = dr("lnp", [4, 128, 32], F32)
    D["wr"] = dr("wr", [4, 1024, 20], F32)
    D["br"] = dr("br", [4, 1, 20], F32)
    D["wge"] = [dr("wge%d" % l, [16, 1024, 512], F32) for l in range(4)]
    D["wue"] = [dr("wue%d" % l, [16, 1024, 512], F32) for l in range(4)]
    D["wde"] = [dr("wde%d" % l, [16, 512, 1024], F32) for l in range(4)]
    D["ident"] = dr("ident", [128, 128], F32)
    D["sel"] = dr("sel", [16, 2048], F32)
    D["xo32T"] = dr("xo32T", [1024, NTOK], F32, "ExternalOutput")
    C = Ctx()
    C.nc = nc; C.D = D
    ymine = [[nc.dram_tensor("ymine%d_%d" % (i, j), [768, 1024], BF16).ap() for j in range(4)] for i in range(2)]
    gy = [[nc.dram_tensor("gy%d_%d" % (i, j), [2 * 768, 1024], BF16).ap() for j in range(4)] for i in range(2)]
    xmine = [[nc.dram_tensor("xmine%d_%d" % (i, j), [1024, 1024], BF16).ap() for j in range(2)] for i in range(2)]
    gx = [[nc.dram_tensor("gx%d_%d" % (i, j), [2 * 1024, 1024], BF16).ap() for j in range(2)] for i in range(2)]
    pv = lambda t: t.rearrange("(c p) t -> p c t", p=128)
    C.ymine_v = [[pv(t) for t in row] for row in ymine]
    C.gy_v = [[[pv(t[s_ * 768:(s_ + 1) * 768, :]) for s_ in range(2)] for t in row] for row in gy]
    C.xmine_v = [[pv(t) for t in row] for row in xmine]
    C.gx_v = [[[pv(t[s_ * 1024:(s_ + 1) * 1024, :]) for s_ in range(2)] for t in row] for row in gx]
    RG = [[2 * i, 2 * i + 1] for i in range(NC_FUSED // 2)]
    S = Sched(nc)
    C.S = S
    C.hymine = [[S.T() for j in range(4)] for i in range(2)]; C.hgy = [[S.T() for j in range(4)] for i in range(2)]
    C.hxmine = [[S.T() for j in range(2)] for i in range(2)]; C.hgx = [[S.T() for j in range(2)] for i in range(2)]
    with ExitStack() as es:
        A = Arena(nc, es, 204 * 1024)
        C.A = A
        C.ps = [es.enter_context(nc.psum_tensor("ps%d" % i, [128, 512], F32)) for i in range(6)]
        C.hps = [S.T("ps%d" % i) for i in range(6)]
        C.ptb = [es.enter_context(nc.psum_tensor("ptb%d" % i, [128, 1024], BF16)) for i in range(2)]
        C.hptb = [S.T("ptb%d" % i) for i in range(2)]
        C.X32 = A.carve([8, NTOK], F32)
        C.hx = [[S.T("x%d_%d" % (c, t)) for t in range(NT)] for c in range(8)]
        C.LNP = A.carve([32], F32); C.hlnp = S.T("lnp")
        C.ONESB = A.carve([128], BF16); C.hones = S.T("ones")
        C.ONES32 = A.carve([128], F32)
        C.SELM = A.carve([2], F32); C.hselm = S.T("selm")
        C.base = A.off
        V(S, "dve", "memset", C.ONESB, 1.0 / 1024.0, writes=[C.hones])
        V(S, "dve", "memset", C.ONES32, 1.0, writes=[C.hones])
        S.dma("sp", C.SELM, D["selm"], writes=[C.hselm])
        x32v = D["x32h"].rearrange("(c p) t -> p c t", p=128)
        for c in range(8):
            for t in range(NT):
                S.dma("sp", C.X32[:, c, t * TT:(t + 1) * TT], x32v[:, c, t * TT:(t + 1) * TT], writes=[C.hx[c][t]])
        for l in range(NL_FUSED):
            emit_M(C, l)
            barrier(S)
            for j in range(4):
                S.add("pool", lambda e, l=l, j=j: e.collective_compute("AllGather", ALU.bypass, RG, [ymine[l % 2][j].opt()], [gy[l % 2][j].opt()]),
                      reads=[C.hymine[l % 2][j]], writes=[C.hgy[l % 2][j]], dma="cc")
            emit_T(C, l)
            barrier(S)
            if l < NL_FUSED - 1:
                for j in range(2):
                    S.add("pool", lambda e, l=l, j=j: e.collective_compute("AllGather", ALU.bypass, RG, [xmine[l % 2][j].opt()], [gx[l % 2][j].opt()]),
                          reads=[C.hxmine[l % 2][j]], writes=[C.hgx[l % 2][j]], dma="cc")
        final_wait(S)
        S.emit()
    print("fused stats", S.stats)
    return nc


def kernel(**inputs):
    inp = {k: np.asarray(v) for k, v in inputs.items()}
    x = inp["x"]
    B = x.shape[0]
    xT = [np.ascontiguousarray(x[b].T).astype(np.float32) for b in range(B)]
    K = [m_consts(hh) for hh in range(2)]
    maps = []
    shared = {}
    for c in range(NC_FUSED):
        b, hh = c // 2, c % 2
        m = {}
        ml = [m_inputs(inp, l, xT[b], hh, K[hh]) for l in range(4)]
        m["x32f"] = xT[b]
        m["x32h"] = np.ascontiguousarray(xT[b][:, hh * 2048:(hh + 1) * 2048])
        sm = np.zeros((128, 2), np.float32); sm[:, hh] = 1.0
        m["selm"] = sm
        for k_ in ("w", "convw", "c32", "poolw"):
            m[k_] = np.stack([ml[l][k_] for l in range(4)], axis=0)
        m["tabs"] = ml[0]["tabs"]; m["cbf"] = ml[0]["cbf"]
        if not shared:
            IN = 7176 - 3072
            f = lambda v: v.reshape(8, 128).T
            shared["wg"] = np.ascontiguousarray(inp["w_in"][:, :, IN:])
            shared["wbr"] = np.ascontiguousarray(inp["w_branch"].reshape(4, 1536, 1024))
            shared["wout"] = np.ascontiguousarray(inp["w_out"])
            shared["lnp"] = np.stack([np.concatenate([f(inp["ln1_g"][l]), f(inp["ln1_b"][l]), f(inp["ln2_g"][l]), f(inp["ln2_b"][l])], axis=1) for l in range(4)], 0).astype(np.float32)
            shared["wr"] = np.ascontiguousarray(np.concatenate([inp["router_coarse_w"], inp["router_fine_w"]], axis=2))
            shared["br"] = np.ascontiguousarray(np.concatenate([inp["router_coarse_b"], inp["router_fine_b"]], axis=1)[:, None, :])
            for l in range(4):
                shared["wge%d" % l] = np.ascontiguousarray(inp["w_gate"][l]); shared["wue%d" % l] = np.ascontiguousarray(inp["w_up"][l])
                shared["wde%d" % l] = np.ascontiguousarray(inp["w_down"][l])
            shared["ident"] = np.eye(128, dtype=np.float32)
            sel = np.zeros((16, 16, 128), np.float32)
            for e in range(16):
                sel[e, e, :] = 1.0
            shared["sel"] = sel.reshape(16, 2048)
        m.update(shared)
        maps.append(m)
    nc = build_fused()
    res = run_bass_kernel_spmd(nc, maps, core_ids=list(range(NC_FUSED)))
    out = np.zeros((B, 4096, 1024), np.float32)
    for c in range(NC_FUSED):
        b, hh = c // 2, c % 2
        out[b, hh * 2048:(hh + 1) * 2048, :] = np.asarray(res.results[c]["xo32T"]).T
    return out
```
